# Optimizing a Trainium2 kernel written in Bass

```python
import jax, jax.numpy as jnp
from jax import lax
import numpy as np

D_MODEL = 1024
BATCH = 4
SEQ = 8192
DEPTH = 2

D_MIX = D_MODEL
D_POOL = D_MIX // 2
D_GLA = D_MIX - D_POOL
POOL_WINDOWS = (2, 4, 8, 16)
N_POOL_GROUPS = len(POOL_WINDOWS)
POOL_GROUP = D_POOL // N_POOL_GROUPS
GLA_HEADS = 4
GLA_DV = D_GLA // GLA_HEADS
GLA_DK = GLA_DV // 2
GATE_RANK = 16
GATE_NORM = 16.0
CHUNK = 64
D_FF = ((8 * D_MODEL // 3 + 127) // 128) * 128
CONV_W = 3
EPS = 1e-6
IN_SIZES = (D_POOL, GLA_HEADS * GLA_DK, GLA_HEADS * GLA_DK, D_GLA, D_GLA, GATE_RANK)
D_IN = sum(IN_SIZES)
IN_SPLITS = [int(s) for s in np.cumsum(IN_SIZES)[:-1]]

kernel_name = "hybrid_pool_gla_convffn_adaln"


def rmsnorm(x):
    xf = x.astype(jnp.float32)
    xf = xf * lax.rsqrt(jnp.mean(xf * xf, axis=-1, keepdims=True) + EPS)
    return xf.astype(x.dtype)


def modulate(h, shift, scale):
    return h * (1 + scale[:, None, :]) + shift[:, None, :]


def pool_mixer(u, w_pool, pool_scale):
    B, S, _ = u.shape
    cs = jnp.cumsum(u.astype(jnp.float32), axis=1).reshape(B, S, N_POOL_GROUPS, POOL_GROUP)
    pos = jnp.arange(1, S + 1, dtype=jnp.float32)
    means = []
    for gi, w in enumerate(POOL_WINDOWS):
        cg = cs[:, :, gi]
        prev = jnp.pad(cg, ((0, 0), (w, 0), (0, 0)))[:, :S]
        cnt = jnp.minimum(pos, float(w))[None, :, None]
        means.append((cg - prev) / cnt)
    mean = jnp.stack(means, axis=2)
    d = (mean - u.reshape(B, S, N_POOL_GROUPS, POOL_GROUP).astype(jnp.float32)).astype(u.dtype)
    y = jnp.einsum('bsgc,gcd->bsgd', d, w_pool)
    return y.reshape(B, S, D_POOL) * pool_scale


def gla_chunk_step(state, inp):
    q, k, v, g = inp
    b = jnp.cumsum(g, axis=2)
    o_inter = jnp.einsum('bhcd,bhde->bhce', q * jnp.exp(b), state)
    diff = b[:, :, :, None, :] - b[:, :, None, :, :]
    mask = jnp.tril(jnp.ones((CHUNK, CHUNK), dtype=bool))[None, None, :, :, None]
    decay = jnp.where(mask, jnp.exp(jnp.minimum(diff, 0.0)), 0.0)
    attn = jnp.einsum('bhid,bhjd,bhijd->bhij', q, k, decay)
    o_intra = jnp.einsum('bhij,bhje->bhie', attn, v)
    b_last = b[:, :, -1, :]
    k_dec = k * jnp.exp(b_last[:, :, None, :] - b)
    new_state = state * jnp.exp(b_last)[..., None] + jnp.einsum('bhcd,bhce->bhde', k_dec, v)
    return new_state, o_inter + o_intra


def gla_mixer(q, k, v, r, z, w_gate2, b_gate, gla_norm):
    B, S, _ = q.shape
    n_chunks = S // CHUNK
    glog = jax.nn.log_sigmoid((z @ w_gate2 + b_gate).astype(jnp.float32)) / GATE_NORM

    def to_chunks(t, d):
        return t.reshape(B, n_chunks, CHUNK, GLA_HEADS, d).transpose(1, 0, 3, 2, 4).astype(jnp.float32)

    qc = to_chunks(q, GLA_DK) * (GLA_DK ** -0.5)
    kc = to_chunks(k, GLA_DK)
    vc = to_chunks(v, GLA_DV)
    gc = to_chunks(glog, GLA_DK)
    s0 = jnp.zeros((B, GLA_HEADS, GLA_DK, GLA_DV), jnp.float32)
    _, o = lax.scan(gla_chunk_step, s0, (qc, kc, vc, gc))
    o = o.transpose(1, 0, 3, 2, 4).reshape(B, S, GLA_HEADS, GLA_DV)
    o = o * lax.rsqrt(jnp.mean(o * o, axis=-1, keepdims=True) + EPS)
    o = o.reshape(B, S, D_GLA).astype(q.dtype) * gla_norm
    return o * jax.nn.silu(r)


def conv_ffn(h, w_up, conv_w, conv_b, w_down):
    up = h @ w_up
    ch = up.shape[-1]
    y = lax.conv_general_dilated(up, conv_w[:, None, :].astype(up.dtype), window_strides=(1,),
                                 padding=((CONV_W - 1, 0),),
                                 dimension_numbers=('NWC', 'WIO', 'NWC'),
                                 feature_group_count=ch) + conv_b
    a, b = jnp.split(y, 2, axis=-1)
    return (jax.nn.silu(a) * b) @ w_down


def setup_inputs(seed: int = 0) -> dict:
    key = jax.random.key(seed)
    ks = jax.random.split(key, 20)
    f32 = jnp.float32
    L = DEPTH

    def nrm(k, shape, scale):
        return jax.random.normal(k, shape, f32) * scale

    return {
        "x": nrm(ks[0], (BATCH, SEQ, D_MODEL), 1.0),
        "c": nrm(ks[1], (BATCH, D_MODEL), 1.0),
        "ada_w": nrm(ks[2], (L, D_MODEL, 6 * D_MODEL), 0.5 * D_MODEL ** -0.5),
        "ada_b": nrm(ks[3], (L, 6 * D_MODEL), 0.02),
        "w_in": nrm(ks[4], (L, D_MODEL, D_IN), D_MODEL ** -0.5),
        "w_gate2": nrm(ks[5], (L, GATE_RANK, GLA_HEADS * GLA_DK), GATE_RANK ** -0.5),
        "b_gate": nrm(ks[6], (L, GLA_HEADS * GLA_DK), 0.02),
        "w_pool": nrm(ks[7], (L, N_POOL_GROUPS, POOL_GROUP, POOL_GROUP), POOL_GROUP ** -0.5),
        "pool_scale": 1.0 + nrm(ks[8], (L, D_POOL), 0.02),
        "gla_norm": 1.0 + nrm(ks[9], (L, D_GLA), 0.02),
        "w_out": nrm(ks[10], (L, D_MIX, D_MODEL), D_MIX ** -0.5),
        "w_up": nrm(ks[11], (L, D_MODEL, 2 * D_FF), D_MODEL ** -0.5),
        "conv_w": nrm(ks[12], (L, CONV_W, 2 * D_FF), CONV_W ** -0.5),
        "conv_b": nrm(ks[13], (L, 2 * D_FF), 0.02),
        "w_down": nrm(ks[14], (L, D_FF, D_MODEL), D_FF ** -0.5),
        "final_norm": 1.0 + nrm(ks[15], (D_MODEL,), 0.02),
    }


def reference(x, c, ada_w, ada_b, w_in, w_gate2, b_gate, w_pool, pool_scale, gla_norm,
              w_out, w_up, conv_w, conv_b, w_down, final_norm):
    c_act = jax.nn.silu(c)
    for l in range(DEPTH):
        mod = c_act @ ada_w[l] + ada_b[l]
        sh1, sc1, g1, sh2, sc2, g2 = jnp.split(mod, 6, axis=-1)

        h = modulate(rmsnorm(x), sh1, sc1)
        u, q, k, v, r, z = jnp.split(h @ w_in[l], IN_SPLITS, axis=-1)
        y_pool = pool_mixer(u, w_pool[l], pool_scale[l])
        y_gla = gla_mixer(q, k, v, r, z, w_gate2[l], b_gate[l], gla_norm[l])
        y = jnp.concatenate([y_pool, y_gla], axis=-1) @ w_out[l]
        x = x + g1[:, None, :] * y

        h = modulate(rmsnorm(x), sh2, sc2)
        x = x + g2[:, None, :] * conv_ffn(h, w_up[l], conv_w[l], conv_b[l], w_down[l])
    return rmsnorm(x) * final_norm
```

```python
import numpy as np
import ml_dtypes
from contextlib import ExitStack
import concourse.bass as bass
import concourse.mybir as mybir
from concourse.bass_utils import run_bass_kernel_spmd

F32 = mybir.dt.float32
BF16 = mybir.dt.bfloat16
AF = mybir.ActivationFunctionType
ALU = mybir.AluOpType

D = 1024
SEQ = 8192
BATCH = 4
DEPTH = 2
T = 256
NB = T // 128
DIN = 2064
DFF = 2816
NPAIR = 22
EPS = 1e-6
GROUPS = (6, 6, 5, 5)
WINDOWS = (2, 4, 8, 16)


class Sched:
    def __init__(self, nc):
        self.nc = nc
        self.e = {"pe": nc.tensor, "act": nc.scalar, "dve": nc.vector, "pool": nc.gpsimd, "sp": nc.sync}
        self.semh = {k: nc.alloc_semaphore(name="s_" + k) for k in self.e}
        self.cnt = {k: 0 for k in self.e}
        self.seen = {k: {} for k in self.e}
        self.lastw = {}
        self.readers = {}

    def _sem(self, sk):
        if sk not in self.semh:
            self.semh[sk] = self.nc.alloc_semaphore(name="d_" + str(len(self.semh)))
            self.cnt[sk] = 0
        return self.semh[sk]

    def _need(self, eng, deps):
        best = {}
        for sk, v in deps:
            if v > best.get(sk, 0):
                best[sk] = v
        for sk, v in best.items():
            if self.seen[eng].get(sk, 0) < v:
                self.e[eng].wait_ge(self.semh[sk], v)
                self.seen[eng][sk] = v

    def _deps(self, reads, writes):
        deps = []
        for k in reads:
            if k in self.lastw:
                deps.append(self.lastw[k])
        for k in writes:
            if k in self.lastw:
                deps.append(self.lastw[k])
            deps.extend(self.readers.get(k, {}).items())
        return deps

    def _commit(self, reads, writes, tag):
        for k in reads:
            r = self.readers.setdefault(k, {})
            if tag[1] > r.get(tag[0], 0):
                r[tag[0]] = tag[1]
        for k in writes:
            self.lastw[k] = tag
            self.readers[k] = {}

    def op(self, eng, fn, reads=(), writes=()):
        self._need(eng, self._deps(reads, writes))
        ins = fn(self.e[eng])
        self.cnt[eng] += 1
        ins.then_inc(self.semh[eng], 1)
        self._commit(reads, writes, (eng, self.cnt[eng]))

    def dma(self, eng, sk, out, in_, reads=(), writes=(), nodep_writes=()):
        self._sem(sk)
        self._need(eng, self._deps(reads, writes))
        ins = self.e[eng].dma_start(out=out, in_=in_)
        self.cnt[sk] += 16
        ins.then_inc(self.semh[sk], 16)
        self._commit(reads, list(writes) + list(nodep_writes), (sk, self.cnt[sk]))

    def custom(self, eng, sk, fn, reads=(), writes=()):
        self._sem(sk)
        self._need(eng, self._deps(reads, writes))
        ins = fn(self.e[eng])
        self.cnt[sk] += 16
        ins.then_inc(self.semh[sk], 16)
        self._commit(reads, writes, (sk, self.cnt[sk]))

    def barrier(self, engines=None):
        for eng in engines or self.e:
            self._need(eng, [(sk, c) for sk, c in self.cnt.items() if c > 0])


def build_program(NT, stop=None, fused=False, groups=None, wave=False):
    S_tok = NT * T
    nc = bass.Bass("TRN2", target_bir_lowering=False)

    def din(name, shape, dt=F32):
        return nc.dram_tensor(name, list(shape), dt, kind="ExternalInput").ap()

    from types import SimpleNamespace
    NL = 2 if wave else 1
    x_d = din("x", [S_tok, D])
    cT_d = din("cT", [128, 8])
    LS = []
    for l_ in range(NL):
        sf = str(l_) if wave else ""
        LS.append(SimpleNamespace(
            l=l_,
            adaw_d=din("ada_w" + sf, [D, 6 * D]), adabT_d=din("ada_bT" + sf, [128, 48]), adabg_d=din("ada_bg" + sf, [1, 2048]),
            win_d=din("w_in" + sf, [D, DIN]), wg2a_d=din("wg2a" + sf, [32, 256]), wpool_d=din("w_pool" + sf, [128, 4, 128]),
            psc_d=din("pool_scT" + sf, [128, 4]), gln_d=din("gla_nT" + sf, [128, 4]), wout_d=din("w_out" + sf, [D, D]),
            wup_d=din("w_up" + sf, [D, 2 * DFF]), cw_d=din("conv_wT" + sf, [128, 132]), cb_d=din("conv_bT" + sf, [128, 44]),
            wdn_d=din("w_down" + sf, [DFF, D])))
    L = SimpleNamespace()

    def setL(l_):
        L.__dict__.update(LS[l_].__dict__)
    fn_d = din("fnrow", [128, D])
    flag_d = din("flags", [128, 4])
    tri_d = din("tri", [128, 128])
    mask_d = din("maskrep", [128, 512])
    ident_d = din("ident", [128, 128])
    invc_d = din("invcnt", [128, 64])
    hm_d = din("hmask", [128, 2])
    out_d = nc.dram_tensor("out", [S_tok, D], F32, kind="ExternalOutput").ap()
    for l_ in range(NL):
        LS[l_].wup_s = nc.dram_tensor("wup_s%d" % l_, [NPAIR, 128, 8, 256], BF16).ap()
        LS[l_].wdn_s = nc.dram_tensor("wdn_s%d" % l_, [NPAIR, 128, 1024], BF16).ap()
        LS[l_].win_s = nc.dram_tensor("win_s%d" % l_, [128, 8 * DIN], BF16).ap()
        LS[l_].wout_s = nc.dram_tensor("wout_s%d" % l_, [128, 8 * D], BF16).ap()
        LS[l_].wpool_s = nc.dram_tensor("wpool_s%d" % l_, [128, 512], BF16).ap()

    S = Sched(nc)
    dbg = {}

    def dump(name, ap, shape, dt=F32, key=None):
        d = nc.dram_tensor("dbg_" + name, list(shape), dt, kind="ExternalOutput").ap()
        S.dma("sp", "dbg_" + name, out=d, in_=ap, reads=[key] if key is not None else [])

    class _Stop(Exception):
        pass
    sb = lambda name, shape, dt=F32: nc.alloc_sbuf_tensor("sb_" + name, list(shape), dt)

    winb = sb("winb", [128, 8, DIN], BF16)
    woutb = sb("woutb", [128, 8, D], BF16)
    wpoolb = sb("wpoolb", [128, 4, 128], BF16)
    for l_ in range(NL):
        q_ = LS[l_]
        n_ = lambda nm: nm + str(l_)
        q_.modT = sb(n_("modT"), [128, 48])
        q_.pscT = sb(n_("pscT"), [128, 4])
        q_.glnT = sb(n_("glnT"), [128, 4])
        q_.cwT = sb(n_("cwT"), [128, 132])
        q_.cbT = sb(n_("cbT"), [128, 44])
        q_.wg2a = sb(n_("wg2a"), [32, 256])
        q_.chalo = sb(n_("chalo"), [128, 44, 2])
        q_.ubuf = sb(n_("ubuf"), [128, 4, 16 + T])
        q_.Sst = sb(n_("Sst"), [128, 2, 256])
    fnrow = sb("fnrow", [128, D])
    flags = sb("flags", [128, 4])
    tri = sb("tri", [128, 128])
    maskrep = sb("maskrep", [128, 512])
    identf = sb("identf", [128, 128])
    identb = sb("identb", [128, 128], BF16)
    invc = sb("invc", [128, 64])
    hm = sb("hm", [128, 2])
    onesD = sb("onesD", [128, 128])
    epsT = sb("epsT", [128, 1])
    zaug = sb("zaug", [32, T])

    pf = [nc.alloc_psum_tensor("pf%d" % i, [128, 512], F32) for i in range(6)]
    pt = [nc.alloc_psum_tensor("pt%d" % i, [128, 1024], BF16) for i in range(2)]
    rr = {"pf": 0, "pt": 0}

    def next_pf():
        i = rr["pf"]
        rr["pf"] = (i + 1) % len(pf)
        return pf[i], ("pf", i)

    def next_pt():
        i = rr["pt"]
        rr["pt"] = (i + 1) % len(pt)
        return pt[i], ("pt", i)

    def load_const(dst, src, key):
        S.dma("sp", "ld_" + key, out=dst, in_=src, writes=[key])

    load_const(fnrow[:, :], fn_d, "fnrow")
    load_const(flags[:, :], flag_d, "flags")
    load_const(tri[:, :], tri_d, "tri")
    load_const(maskrep[:, :], mask_d, "maskrep")
    load_const(identf[:, :], ident_d, "identf")
    load_const(invc[:, :], invc_d, "invc")
    load_const(hm[:, :], hm_d, "hm")
    S.op("dve", lambda e: e.tensor_copy(identb[:, :], identf[:, :]), reads=["identf"], writes=["identb"])
    S.op("dve", lambda e: e.memset(onesD[:, :], 1.0 / 128.0), writes=["onesD"])
    S.op("dve", lambda e: e.memset(epsT[:, :], EPS), writes=["epsT"])
    for l_ in range(NL):
        setL(l_)
        S.dma("sp", "ld_pscT%d" % l_, out=L.pscT[:, :], in_=L.psc_d, writes=[("pscT", L.l)])
        S.dma("sp", "ld_glnT%d" % l_, out=L.glnT[:, :], in_=L.gln_d, writes=[("glnT", L.l)])
        S.dma("sp", "ld_cwT%d" % l_, out=L.cwT[:, :], in_=L.cw_d, writes=[("cwT", L.l)])
        S.dma("sp", "ld_cbT%d" % l_, out=L.cbT[:, :], in_=L.cb_d, writes=[("cbT", L.l)])
        S.dma("sp", "ld_wg2a%d" % l_, out=L.wg2a[:, :], in_=L.wg2a_d, writes=[("wg2a", L.l)])
        S.op("dve", lambda e: e.memset(L.chalo[:, :, :], 0.0), writes=[("chalo", L.l, j) for j in range(44)])
        S.op("dve", lambda e: e.memset(L.ubuf[:, :, :], 0.0), writes=[("ub", L.l, g) for g in range(4)])
        S.op("dve", lambda e: e.memset(L.Sst[:, :, :], 0.0), writes=[("Sst", L.l)])
    S.op("dve", lambda e: e.memset(zaug[:, :], 1.0), writes=["zaug"])

    with ExitStack() as ps_:
        pb = lambda name, shape, dt=F32: ps_.enter_context(nc.sbuf_tensor("pb_" + name, list(shape), dt))
        cT = pb("cT", [128, 8])
        cact = pb("cact", [128, 8])
        ones128 = pb("ones128", [128, 128])
        crep = pb("crep", [128, 8, 128])
        adabT = pb("adabT", [128, 48])
        adabg = pb("adabg", [1, 2048])
        Gt = [pb("G1t", [128, D]), pb("G2t", [128, D])]
        ablk = [pb("ablk%d" % i, [128, 8, 512]) for i in range(2)]
        stg = [pb("stg%d" % i, [128, 2 * DFF]) for i in range(2)]
        cvt = [pb("cvt%d" % i, [128, 2 * DFF], BF16) for i in range(2)]

        load_const(cT[:, :], cT_d, "cT")
        S.op("act", lambda e: e.activation(out=cact[:, :], in_=cT[:, :], func=AF.Silu), reads=["cT"], writes=["cact"])
        S.op("dve", lambda e: e.memset(ones128[:, :], 1.0), writes=["ones128"])
        for k in range(8):
            S.op("dve", lambda e: e.tensor_scalar(out=crep[:, k, :], in0=ones128[:, :], scalar1=cact[:, k:k + 1],
                                                  scalar2=None, op0=ALU.mult),
                 reads=["ones128", "cact"], writes=[("crep", k)])
        for l_ in range(NL):
            setL(l_)
            S.dma("sp", "ld_adabT", out=adabT[:, :], in_=L.adabT_d, writes=["adabT"])
            S.dma("sp", "ld_adabg", out=adabg[:, :], in_=L.adabg_d, writes=["adabg"])
            for nb in range(12):
                a = ablk[nb % 2]
                ak = ("ablk", nb % 2)
                S.dma("sp", "ld_ablk%d" % (nb % 2), out=a[:, :, :],
                      in_=L.adaw_d[:, nb * 512:(nb + 1) * 512].rearrange("(k p) n -> p k n", p=128), writes=[ak])
                ps, pk = next_pf()
                if nb in (4, 5, 10, 11):
                    gi = 0 if nb < 6 else 1
                    half = nb % 2
                    boff = (0 if nb < 6 else 1024) + half * 512

                    def f(e):
                        for k in range(8):
                            e.matmul(ps[:, :], lhsT=crep[:, k, :], rhs=a[:, k, :], start=(k == 0), stop=False)
                        return e.matmul(ps[:, :], lhsT=ones128[0:1, :], rhs=adabg[0:1, boff:boff + 512], start=False, stop=True)
                    S.op("pe", f, reads=[ak, "ones128", "adabg"] + [("crep", k) for k in range(8)], writes=[pk])
                    S.op("act", lambda e: e.activation(out=Gt[gi][:, half * 512:(half + 1) * 512], in_=ps[:, :], func=AF.Copy),
                         reads=[pk], writes=[("Gt", gi, half)])
                else:
                    def f(e):
                        for m in range(4):
                            for k in range(8):
                                ins = e.matmul(ps[:, m:m + 1], lhsT=a[:, k, m * 128:(m + 1) * 128], rhs=cact[:, k:k + 1],
                                               start=(k == 0), stop=(k == 7))
                        return ins
                    S.op("pe", f, reads=[ak, "cact"], writes=[pk])
                    S.op("dve", lambda e: e.tensor_tensor(out=L.modT[:, 4 * nb:4 * nb + 4], in0=ps[:, 0:4],
                                                          in1=adabT[:, 4 * nb:4 * nb + 4], op=ALU.add),
                         reads=[pk, "adabT"], writes=[("modT", L.l, nb)])
            for nb in (2, 3, 8, 9):
                S.op("dve", lambda e: e.tensor_scalar(out=L.modT[:, 4 * nb:4 * nb + 4], in0=L.modT[:, 4 * nb:4 * nb + 4],
                                                      scalar1=1.0, scalar2=None, op0=ALU.add),
                     reads=[("modT", L.l, nb)], writes=[("modT", L.l, nb)])
            ci = [0]

            def stage_load(src, ncols):
                i = ci[0] % 2
                ci[0] += 1
                S.dma("sp", "ld_stg%d" % i, out=stg[i][:, 0:ncols], in_=src, writes=[("stg", i)])
                return i

            for k in range(8):
                i = stage_load(L.win_d[k * 128:(k + 1) * 128, :], DIN)
                eng = "act" if k % 2 == 0 else "dve"
                if eng == "act":
                    S.op("act", lambda e: e.activation(out=winb[:, k, :], in_=stg[i][:, 0:DIN], func=AF.Copy),
                         reads=[("stg", i)], writes=["winb"])
                else:
                    S.op("dve", lambda e: e.tensor_copy(winb[:, k, :], stg[i][:, 0:DIN]), reads=[("stg", i)], writes=["winb"])
            for k in range(8):
                i = stage_load(L.wout_d[k * 128:(k + 1) * 128, :], D)
                S.op("dve", lambda e: e.tensor_tensor(out=woutb[:, k, :], in0=stg[i][:, 0:D], in1=Gt[0][:, :], op=ALU.mult),
                     reads=[("stg", i), ("Gt", 0, 0), ("Gt", 0, 1)], writes=["woutb"])
            i = stage_load(L.wpool_d.rearrange("p g d -> p (g d)"), 512)
            S.op("dve", lambda e: e.tensor_copy(wpoolb[:, :, :].rearrange("p g d -> p (g d)"), stg[i][:, 0:512]),
                 reads=[("stg", i)], writes=["wpoolb"])
            for k in range(8):
                i = stage_load(L.wup_d[k * 128:(k + 1) * 128, :], 2 * DFF)
                c3 = cvt[i][:, :].rearrange("p (i c) -> p i c", c=256)
                S.op("act", lambda e: e.activation(out=c3[:, :, 0:128], in_=stg[i][:, 0:DFF].rearrange("p (i c) -> p i c", c=128),
                                                   func=AF.Copy), reads=[("stg", i)], writes=[("cvt", i, 0)])
                S.op("dve", lambda e: e.tensor_copy(c3[:, :, 128:256], stg[i][:, DFF:2 * DFF].rearrange("p (i c) -> p i c", c=128)),
                     reads=[("stg", i)], writes=[("cvt", i, 1)])
                for hh in range(2):
                    S.dma("sp", "st_scr%d" % i, out=L.wup_s[hh * 11:(hh + 1) * 11, :, k, :].rearrange("i p c -> p i c"),
                          in_=c3[:, hh * 11:(hh + 1) * 11, :], reads=[("cvt", i, 0), ("cvt", i, 1)], nodep_writes=["scratch%d" % i])
            for j in range(NPAIR):
                i = stage_load(L.wdn_d[j * 128:(j + 1) * 128, :], D)
                S.op("dve", lambda e: e.tensor_tensor(out=cvt[i][:, 0:D], in0=stg[i][:, 0:D], in1=Gt[1][:, :], op=ALU.mult),
                     reads=[("stg", i), ("Gt", 1, 0), ("Gt", 1, 1)], writes=[("cvt", i, 0), ("cvt", i, 1)])
                S.dma("sp", "st_scr%d" % i, out=L.wdn_s[j], in_=cvt[i][:, 0:D],
                      reads=[("cvt", i, 0), ("cvt", i, 1)], nodep_writes=["scratch%d" % i])
            if wave:
                S.dma("sp", "st_mw0", out=L.win_s, in_=winb[:, :, :].rearrange("p k n -> p (k n)"), reads=["winb"],
                      nodep_writes=[("mws", L.l, 0)])
                S.dma("sp", "st_mw1", out=L.wout_s, in_=woutb[:, :, :].rearrange("p k n -> p (k n)"), reads=["woutb"],
                      nodep_writes=[("mws", L.l, 1)])
                S.dma("sp", "st_mw2", out=L.wpool_s, in_=wpoolb[:, :, :].rearrange("p g d -> p (g d)"), reads=["wpoolb"],
                      nodep_writes=[("mws", L.l, 2)])
        S.barrier()
        if stop == "prep":
            S.barrier()
            dump("winb", winb[:, :, :], [128, 8, DIN], BF16)
            dump("woutb", woutb[:, :, :], [128, 8, D], BF16)
            dump("modT", L.modT[:, :], [128, 48])
            dump("G1t", Gt[0][:, :], [128, D])
            S.barrier()
            return nc

    xts = [sb("xt%d" % i, [128, NB, D]) for i in range(1 if fused else 2)]
    xnb = sb("xnb", [128, NB, D], BF16)
    junk = sb("junk", [128, D], BF16)
    hT = sb("hT", [128, 8, T], BF16)
    ss = sb("ss", [128, 4])
    rstd = sb("rstd", [128, 4])
    sfin = sb("sfin", [128, 4])
    wups = [sb("wups%d" % i, [128, 8, 256], BF16) for i in range(3)]
    wdng = [sb("wdng%d" % i, [128, 6, D], BF16) for i in range(2)]
    hidg = [sb("hidg%d" % i, [128, 6, T], BF16) for i in range(2)]
    pA = sb("pA", [128, 16 + T])
    pB = sb("pB", [128, 16 + T])
    tmp16 = sb("tmp16", [128, 16])
    dT = sb("dT", [128, 4, T], BF16)
    qT = sb("qT", [128, 2, T])
    kT = sb("kT", [128, 2, T])
    srT = sb("srT", [128, 4, T])
    vb = sb("vb", [128, NB, 512], BF16)
    gtm = sb("gtm", [128, NB, 256])
    e1 = sb("e1", [128, 2, T])
    e2 = sb("e2", [128, 2, T])
    pbias = sb("pbias", [128, 2 * NB])
    nbias = sb("nbias", [128, 2 * NB])
    em = sb("em", [128, 2 * NB])
    dl = sb("dl", [128, 2 * NB])
    qpp = sb("qpp", [128, 4, T], BF16)
    kpp = sb("kpp", [128, 2, T], BF16)
    kptok = sb("kptok", [128, NB, 256], BF16)
    am = sb("am", [128, 512], BF16)
    Smb = sb("Smb", [128, 2, 256], BF16)
    tmpU = sb("tmpU", [128, 2, 256])
    oT = sb("oT", [128, 4, T])
    osq = sb("osq", [128, 4, T])
    rso = sb("rso", [128, 4, T])
    otmp = osq
    ymix = sb("ymix", [128, 8, T], BF16)
    upb_ = [sb("up%d" % i, [128, 2 + T]) for i in range(4)]
    tcv = [sb("tcv%d" % i, [128, T]) for i in range(4)]
    sact = [sb("sact%d" % i, [128, T]) for i in range(2)]

    def load_x(t):
        S.dma("sp", "ld_x%d" % (t % 2), out=xts[t % 2][:, :, :],
              in_=x_d[t * T:(t + 1) * T, :].rearrange("(b p) f -> p b f", p=128), writes=[("xt", t % 2)])

    def norm_stats(xt, xk):
        S.op("dve", lambda e: e.memset(ss[:, :], 0.0), writes=["ss"])
        for blk in range(NB):
            S.op("act", lambda e: e.activation(out=junk[:, :], in_=xt[:, blk, :], func=AF.Square,
                                               accum_out=ss[:, blk:blk + 1]), reads=[xk], writes=["junk", "ss"])
        S.op("act", lambda e: e.activation(out=rstd[:, 0:NB], in_=ss[:, 0:NB], func=AF.Sqrt, scale=1.0 / D, bias=epsT[:, 0:1]),
             reads=["ss", "epsT"], writes=["rstd"])
        S.op("dve", lambda e: e.reciprocal(rstd[:, 0:NB], rstd[:, 0:NB]), reads=["rstd"], writes=["rstd"])

    def norm_to_hT(xt, xk, sh_off, sc_off):
        norm_stats(xt, xk)
        for blk in range(NB):
            S.op("dve", lambda e: e.tensor_scalar(out=xnb[:, blk, :], in0=xt[:, blk, :], scalar1=rstd[:, blk:blk + 1],
                                                  scalar2=None, op0=ALU.mult),
                 reads=[xk, "rstd"], writes=[("xnb", blk)])
        for k in range(8):
            p, pk = next_pt()

            def f(e):
                for blk in range(NB):
                    ins = e.transpose(out=p[:, blk * 128:(blk + 1) * 128], in_=xnb[:, blk, k * 128:(k + 1) * 128],
                                      identity=identb[:, :])
                return ins
            S.op("pe", f, reads=[("xnb", b) for b in range(NB)] + ["identb"], writes=[pk])
            S.op("act", lambda e: e.activation(out=hT[:, k, :], in_=p[:, 0:T], func=AF.Identity,
                                               scale=L.modT[:, sc_off + k:sc_off + k + 1],
                                               bias=L.modT[:, sh_off + k:sh_off + k + 1]),
                 reads=[pk] + [("modT", L.l, i) for i in range(12)], writes=[("hT", k)])

    hT_all = [("hT", k) for k in range(8)]

    def proj_fm(col, M):
        ps, pk = next_pf()

        def f(e):
            for k in range(8):
                ins = e.matmul(ps[0:M, 0:T], lhsT=winb[:, k, col:col + M], rhs=hT[:, k, :], start=(k == 0), stop=(k == 7))
            return ins
        S.op("pe", f, reads=hT_all + ["winb"], writes=[pk])
        return ps, pk

    def mixer(t, xt, xk):
        norm_to_hT(xt, xk, 0, 8)
        for g in range(4):
            ps, pk = proj_fm(g * 128, 128)
            S.op("dve", lambda e: e.tensor_copy(L.ubuf[:, g, 16:16 + T], ps[:, 0:T]), reads=[pk], writes=[("ub", L.l, g)])
        for c in range(2):
            ps, pk = proj_fm(512 + c * 128, 128)
            S.op("dve", lambda e: e.tensor_scalar(out=qT[:, c, :], in0=ps[:, 0:T], scalar1=0.125, scalar2=None, op0=ALU.mult),
                 reads=[pk], writes=[("qT", c)])
        for c in range(2):
            ps, pk = proj_fm(768 + c * 128, 128)
            S.op("dve", lambda e: e.tensor_copy(kT[:, c, :], ps[:, 0:T]), reads=[pk], writes=[("kT", c)])
        for c in range(4):
            ps, pk = proj_fm(1536 + c * 128, 128)
            S.op("act", lambda e: e.activation(out=srT[:, c, :], in_=ps[:, 0:T], func=AF.Silu), reads=[pk], writes=[("srT", c)])
        ps, pk = proj_fm(2048, 16)
        S.op("dve", lambda e: e.tensor_copy(zaug[0:16, :], ps[0:16, 0:T]), reads=[pk], writes=["zaug"])
        for blk in range(NB):
            ps, pk = next_pf()

            def f(e):
                for k in range(8):
                    ins = e.matmul(ps[:, :], lhsT=hT[:, k, blk * 128:(blk + 1) * 128], rhs=winb[:, k, 1024:1536],
                                   start=(k == 0), stop=(k == 7))
                return ins
            S.op("pe", f, reads=hT_all + ["winb"], writes=[pk])
            S.op("act", lambda e: e.activation(out=vb[:, blk, :], in_=ps[:, :], func=AF.Copy), reads=[pk], writes=[("vb", blk)])

        if stop == "inproj":
            raise _Stop()
        W = 16 + T
        for g, w in enumerate(WINDOWS):
            ub = L.ubuf[:, g, :]
            uk = ("ub", L.l, g)
            add = lambda o, a, b_: (lambda e: e.tensor_tensor(out=o, in0=a, in1=b_, op=ALU.add))
            S.op("pool", add(pA[:, 1:W], ub[:, 1:W], ub[:, 0:W - 1]), reads=[uk], writes=["pA"])
            cur, ck = pA, "pA"
            if w >= 4:
                S.op("pool", add(pB[:, 3:W], pA[:, 3:W], pA[:, 1:W - 2]), reads=["pA"], writes=["pB"])
                cur, ck = pB, "pB"
            if w >= 8:
                S.op("pool", add(pA[:, 7:W], pB[:, 7:W], pB[:, 3:W - 4]), reads=["pB"], writes=["pA"])
                cur, ck = pA, "pA"
            if w >= 16:
                S.op("pool", add(pB[:, 15:W], pA[:, 15:W], pA[:, 7:W - 8]), reads=["pA"], writes=["pB"])
                cur, ck = pB, "pB"
            oth = pB if cur is pA else pA
            ok_ = "pB" if cur is pA else "pA"
            S.op("pool", lambda e: e.tensor_scalar(out=oth[:, 16:W], in0=cur[:, 16:W], scalar1=1.0 / w, scalar2=None, op0=ALU.mult),
                 reads=[ck], writes=[ok_])
            S.op("pool", lambda e: e.tensor_tensor(out=dT[:, g, :], in0=oth[:, 16:W], in1=ub[:, 16:W], op=ALU.subtract),
                 reads=[ok_, uk], writes=[("dT", g)])
            if t == 0 or (fused and t == 1):
                cf, cfk = (invc, "invc") if t == 0 else (coefB, ("coefB", g))
                S.op("pool", lambda e: e.tensor_tensor(out=tmp16[:, :], in0=cur[:, 16:32], in1=cf[:, g * 16:(g + 1) * 16],
                                                       op=ALU.mult), reads=[ck, cfk], writes=["tmp16"])
                S.op("pool", lambda e: e.tensor_tensor(out=dT[:, g, 0:16], in0=tmp16[:, :], in1=ub[:, 16:32], op=ALU.subtract),
                     reads=["tmp16", uk], writes=[("dT", g)])
            S.op("pool", lambda e: e.tensor_copy(ub[:, 0:16], ub[:, T:T + 16]), reads=[uk], writes=[uk])
            ps, pk = next_pf()
            S.op("pe", lambda e: e.matmul(ps[:, 0:T], lhsT=wpoolb[:, g, :], rhs=dT[:, g, :], start=True, stop=True),
                 reads=[("dT", g), "wpoolb"], writes=[pk])
            S.op("act", lambda e: e.activation(out=ymix[:, g, :], in_=ps[:, 0:T], func=AF.Identity, scale=L.pscT[:, g:g + 1]),
                 reads=[pk, ("pscT", L.l)], writes=[("ymix", g)])

        if stop == "pool":
            raise _Stop()
        ps, pk = next_pf()

        def f(e):
            for blk in range(NB):
                ins = e.matmul(ps[:, blk * 256:(blk + 1) * 256], lhsT=zaug[0:32, blk * 128:(blk + 1) * 128], rhs=L.wg2a[0:32, :],
                               start=True, stop=True)
            return ins
        S.op("pe", f, reads=["zaug", ("wg2a", L.l)], writes=[pk])
        gflat = gtm[:, :, :].rearrange("p b c -> p (b c)")
        S.op("act", lambda e: e.activation(out=gflat, in_=ps[:, 0:NB * 256], func=AF.Exp, scale=-1.0), reads=[pk], writes=["gtm"])
        S.op("act", lambda e: e.activation(out=gflat, in_=gflat, func=AF.Ln, bias=1.0), reads=["gtm"], writes=["gtm"])
        if stop == "gate":
            raise _Stop()
        cps, ck_ = next_pf()

        def f(e):
            for fc in range(2):
                for blk in range(NB):
                    ins = e.matmul(cps[:, fc * T + blk * 128:fc * T + (blk + 1) * 128],
                                   lhsT=gtm[:, blk, fc * 128:(fc + 1) * 128], rhs=tri[:, :], start=True, stop=True)
            return ins
        S.op("pe", f, reads=["gtm", "tri"], writes=[ck_])
        c4 = cps[:, :].rearrange("p (a c) -> p a c", c=128)
        S.op("dve", lambda e: e.tensor_scalar(out=pbias[:, :], in0=c4[:, :, 63], scalar1=1.0 / 16, scalar2=None, op0=ALU.mult),
             reads=[ck_], writes=["pbias"])
        S.op("dve", lambda e: e.tensor_scalar(out=nbias[:, :], in0=c4[:, :, 63], scalar1=-1.0 / 16, scalar2=None, op0=ALU.mult),
             reads=[ck_], writes=["nbias"])
        S.op("act", lambda e: e.activation(out=em[:, :], in_=c4[:, :, 63], func=AF.Exp, scale=-1.0 / 16), reads=[ck_], writes=["em"])
        S.op("act", lambda e: e.activation(out=dl[:, :], in_=c4[:, :, 127], func=AF.Exp, scale=-1.0 / 16), reads=[ck_], writes=["dl"])
        for fc in range(2):
            for blk in range(NB):
                a = fc * NB + blk
                cs = slice(blk * 128, (blk + 1) * 128)
                S.op("act", lambda e: e.activation(out=e1[:, fc, cs], in_=c4[:, a, :], func=AF.Exp, scale=-1.0 / 16,
                                                   bias=pbias[:, a:a + 1]), reads=[ck_, "pbias"], writes=[("e1", fc)])
                S.op("act", lambda e: e.activation(out=e2[:, fc, cs], in_=c4[:, a, :], func=AF.Exp, scale=1.0 / 16,
                                                   bias=nbias[:, a:a + 1]), reads=[ck_, "nbias"], writes=[("e2", fc)])
            for hl in range(2):
                S.op("dve", lambda e: e.scalar_tensor_tensor(out=qpp[:, 2 * fc + hl, :], in0=qT[:, fc, :], scalar=hm[:, hl:hl + 1],
                                                             in1=e1[:, fc, :], op0=ALU.mult, op1=ALU.mult),
                     reads=[("qT", fc), ("e1", fc), "hm"], writes=[("qpp", 2 * fc + hl)])
            S.op("dve", lambda e: e.tensor_tensor(out=kpp[:, fc, :], in0=kT[:, fc, :], in1=e2[:, fc, :], op=ALU.mult),
                 reads=[("kT", fc), ("e2", fc)], writes=[("kpp", fc)])
        if stop == "cum":
            raise _Stop()
        p, pk = next_pt()

        def f(e):
            for blk in range(NB):
                for fc in range(2):
                    ins = e.transpose(out=p[:, (blk * 2 + fc) * 128:(blk * 2 + fc + 1) * 128],
                                      in_=kpp[:, fc, blk * 128:(blk + 1) * 128], identity=identb[:, :])
            return ins
        S.op("pe", f, reads=[("kpp", 0), ("kpp", 1), "identb"], writes=[pk])
        S.op("dve", lambda e: e.tensor_copy(kptok[:, :, :].rearrange("p b c -> p (b c)"), p[:, 0:NB * 256]),
             reads=[pk], writes=["kptok"])
        if stop == "ktok":
            raise _Stop()
        for blk in range(NB):
            cs = slice(blk * 128, (blk + 1) * 128)
            for fc in range(2):
                a = fc * NB + blk
                S.op("dve", lambda e: e.tensor_scalar(out=Smb[:, fc, :], in0=L.Sst[:, fc, :], scalar1=em[:, a:a + 1], scalar2=None,
                                                      op0=ALU.mult), reads=[("Sst", L.l), "em"], writes=[("Smb", fc)])
            aps, ak = next_pf()

            def f(e):
                for h in range(4):
                    fc, hl = h // 2, h % 2
                    rs = slice(64 * hl, 64 * hl + 64)
                    ins = e.matmul(aps[:, h * 128:(h + 1) * 128], lhsT=kpp[:, fc, cs], rhs=qpp[:, h, cs], start=True, stop=True)
                return ins
            S.op("pe", f, reads=[("kpp", 0), ("kpp", 1)] + [("qpp", h) for h in range(4)], writes=[ak])
            S.op("dve", lambda e: e.tensor_tensor(out=am[:, :], in0=aps[:, :], in1=maskrep[:, :], op=ALU.mult),
                 reads=[ak, "maskrep"], writes=["am"])
            ops_, ok = next_pf()

            def f(e):
                for h in range(4):
                    fc, hl = h // 2, h % 2
                    rs = slice(64 * hl, 64 * hl + 64)
                    e.matmul(ops_[:, h * 128:(h + 1) * 128], lhsT=vb[:, blk, h * 128:(h + 1) * 128], rhs=am[:, h * 128:(h + 1) * 128],
                             start=True, stop=False)
                    ins = e.matmul(ops_[:, h * 128:(h + 1) * 128], lhsT=Smb[:, fc, hl * 128:(hl + 1) * 128], rhs=qpp[:, h, cs],
                                   start=False, stop=True)
                return ins
            S.op("pe", f, reads=[("vb", blk), "am", ("Smb", 0), ("Smb", 1)] + [("qpp", h) for h in range(4)], writes=[ok])
            S.op("act", lambda e: e.activation(out=oT[:, :, cs], in_=ops_[:, :].rearrange("p (h c) -> p h c", c=128), func=AF.Copy),
                 reads=[ok], writes=["oT"])
            ups, uk_ = next_pf()

            def f(e):
                for fc in range(2):
                    ins = e.matmul(ups[:, fc * 256:(fc + 1) * 256], lhsT=kptok[:, blk, fc * 128:(fc + 1) * 128],
                                   rhs=vb[:, blk, fc * 256:(fc + 1) * 256], start=True, stop=True)
                return ins
            S.op("pe", f, reads=["kptok", ("vb", blk)], writes=[uk_])
            for fc in range(2):
                a = fc * NB + blk
                S.op("dve", lambda e: e.tensor_scalar(out=tmpU[:, fc, :], in0=ups[:, fc * 256:(fc + 1) * 256],
                                                      scalar1=e1[:, fc, blk * 128 + 127:blk * 128 + 128], scalar2=None, op0=ALU.mult),
                     reads=[uk_, ("e1", fc)], writes=[("tmpU", fc)])
                S.op("dve", lambda e: e.scalar_tensor_tensor(out=L.Sst[:, fc, :], in0=L.Sst[:, fc, :], scalar=dl[:, a:a + 1],
                                                             in1=tmpU[:, fc, :], op0=ALU.mult, op1=ALU.add),
                     reads=[("Sst", L.l), "dl", ("tmpU", fc)], writes=[("Sst", L.l)])
        if stop == "rec":
            raise _Stop()
        S.op("act", lambda e: e.activation(out=osq[:, :, :].rearrange("p h t -> p (h t)"),
                                           in_=oT[:, :, :].rearrange("p h t -> p (h t)"), func=AF.Square), reads=["oT"], writes=["osq"])
        for hp in range(2):
            ps, pk = next_pf()

            def f(e):
                for j in range(2):
                    ins = e.matmul(ps[:, j * T:(j + 1) * T], lhsT=onesD[:, :], rhs=osq[:, 2 * hp + j, :], start=True, stop=True)
                return ins
            S.op("pe", f, reads=["osq", "onesD"], writes=[pk])
            rv = rso[:, 2 * hp:2 * hp + 2, :].rearrange("p h t -> p (h t)")
            S.op("act", lambda e: e.activation(out=rv, in_=ps[:, 0:2 * T], func=AF.Sqrt, bias=epsT[:, 0:1]), reads=[pk, "epsT"],
                 writes=[("rso", hp)])
            S.op("dve", lambda e: e.reciprocal(rv, rv), reads=[("rso", hp)], writes=[("rso", hp)])
        for h in range(4):
            S.op("dve", lambda e: e.scalar_tensor_tensor(out=otmp[:, h, :], in0=oT[:, h, :], scalar=L.glnT[:, h:h + 1], in1=rso[:, h, :],
                                                         op0=ALU.mult, op1=ALU.mult), reads=["oT", ("glnT", L.l), ("rso", h // 2)],
                 writes=["osq"])
            S.op("pool", lambda e: e.tensor_tensor(out=ymix[:, 4 + h, :], in0=otmp[:, h, :], in1=srT[:, h, :], op=ALU.mult),
                 reads=["osq", ("srT", h)], writes=[("ymix", 4 + h)])
        if stop == "onorm":
            raise _Stop()
        for blk in range(NB):
            for half in range(2):
                ps, pk = next_pf()

                def f(e):
                    for kc in range(8):
                        ins = e.matmul(ps[:, :], lhsT=ymix[:, kc, blk * 128:(blk + 1) * 128], rhs=woutb[:, kc, half * 512:(half + 1) * 512],
                                       start=(kc == 0), stop=(kc == 7))
                    return ins
                S.op("pe", f, reads=[("ymix", i) for i in range(8)] + ["woutb"], writes=[pk])
                S.op("dve", lambda e: e.tensor_tensor(out=xt[:, blk, half * 512:(half + 1) * 512], in0=xt[:, blk, half * 512:(half + 1) * 512],
                                                      in1=ps[:, :], op=ALU.add), reads=[pk, xk], writes=[xk])

    wctr = {"up": 0, "dn": 0}

    def ffn(t, xt, xk):
        norm_to_hT(xt, xk, 24, 32)
        pend = []

        def issue_up(i):
            s = wctr["up"] % 3
            wctr["up"] += 1
            S.dma("sp", "ld_wup%d" % s, out=wups[s][:, :, :], in_=L.wup_s[i], reads=["scratch0", "scratch1"], writes=[("wups", s)])
            return s

        def issue_dn(i0, gsz):
            s = wctr["dn"] % 2
            wctr["dn"] += 1
            S.dma("sp", "ld_wdn%d" % s, out=wdng[s][:, 0:gsz, :], in_=L.wdn_s[i0:i0 + gsz].rearrange("i p n -> p i n"),
                  reads=["scratch0", "scratch1"], writes=[("wdng", s)])
            return s

        slots = {}
        slots[0] = issue_up(0)
        slots[1] = issue_up(1)
        i0 = 0
        for gi, gsz in enumerate(GROUPS):
            ds = issue_dn(i0, gsz)
            hs = gi % 2
            for jj in range(gsz):
                i = i0 + jj
                if i + 2 < NPAIR:
                    slots[i + 2] = issue_up(i + 2)
                s = slots[i]
                res = []
                for ab in range(2):
                    j = i + ab * NPAIR
                    ps, pk = next_pf()

                    def f(e):
                        for k in range(8):
                            ins = e.matmul(ps[:, 0:T], lhsT=wups[s][:, k, ab * 128:(ab + 1) * 128], rhs=hT[:, k, :],
                                           start=(k == 0), stop=(k == 7))
                        return ins
                    S.op("pe", f, reads=hT_all + [("wups", s)], writes=[pk])
                    ui = (2 * i + ab) % 4
                    up = upb_[ui]
                    upk = ("up", ui)
                    tc_ = tcv[ui]
                    tk = ("tcv", ui)
                    S.op("pool", lambda e: e.tensor_copy(up[:, 0:2], L.chalo[:, j, :]), reads=[("chalo", L.l, j)], writes=[upk])
                    S.op("act", lambda e: e.activation(out=up[:, 2:2 + T], in_=ps[:, 0:T], func=AF.Copy), reads=[pk], writes=[upk])
                    S.op("pool", lambda e: e.tensor_copy(L.chalo[:, j, :], up[:, T:T + 2]), reads=[upk], writes=[("chalo", L.l, j)])
                    S.op("dve", lambda e: e.tensor_scalar(out=tc_[:, :], in0=up[:, 2:2 + T], scalar1=L.cwT[:, 3 * j + 2:3 * j + 3],
                                                          scalar2=L.cbT[:, j:j + 1], op0=ALU.mult, op1=ALU.add),
                         reads=[upk, ("cwT", L.l), ("cbT", L.l)], writes=[tk])
                    S.op("dve", lambda e: e.scalar_tensor_tensor(out=tc_[:, :], in0=up[:, 1:1 + T], scalar=L.cwT[:, 3 * j + 1:3 * j + 2],
                                                                 in1=tc_[:, :], op0=ALU.mult, op1=ALU.add),
                         reads=[upk, tk, ("cwT", L.l)], writes=[tk])
                    S.op("dve", lambda e: e.scalar_tensor_tensor(out=tc_[:, :], in0=up[:, 0:T], scalar=L.cwT[:, 3 * j:3 * j + 1],
                                                                 in1=tc_[:, :], op0=ALU.mult, op1=ALU.add),
                         reads=[upk, tk, ("cwT", L.l)], writes=[tk])
                    res.append((tc_, tk))
                sa = sact[i % 2]
                sk_ = ("sact", i % 2)
                S.op("act", lambda e: e.activation(out=sa[:, :], in_=res[0][0][:, :], func=AF.Silu), reads=[res[0][1]], writes=[sk_])
                S.op("pool", lambda e: e.tensor_tensor(out=hidg[hs][:, jj, :], in0=sa[:, :], in1=res[1][0][:, :], op=ALU.mult),
                     reads=[sk_, res[1][1]], writes=[("hidg", hs)])
            for blk in range(NB):
                for half in range(2):
                    ps, pk = next_pf()

                    def f(e):
                        for jj in range(gsz):
                            ins = e.matmul(ps[:, :], lhsT=hidg[hs][:, jj, blk * 128:(blk + 1) * 128],
                                           rhs=wdng[ds][:, jj, half * 512:(half + 1) * 512], start=(jj == 0), stop=(jj == gsz - 1))
                        return ins
                    S.op("pe", f, reads=[("hidg", hs), ("wdng", ds)], writes=[pk])
                    S.op("dve", lambda e: e.tensor_tensor(out=xt[:, blk, half * 512:(half + 1) * 512],
                                                          in0=xt[:, blk, half * 512:(half + 1) * 512], in1=ps[:, :], op=ALU.add),
                         reads=[pk, xk], writes=[xk])
            i0 += gsz

    def finish(t, xt, xk, st_tile=-2):
        if st_tile == -2:
            st_tile = t
        norm_stats(xt, xk)
        S.op("dve", lambda e: e.tensor_scalar(out=sfin[:, 0:NB], in0=rstd[:, 0:NB], scalar1=flags[:, 0:1], scalar2=flags[:, 1:2],
                                              op0=ALU.mult, op1=ALU.add), reads=["rstd", "flags"], writes=["sfin"])
        for blk in range(NB):
            S.op("dve", lambda e: e.scalar_tensor_tensor(out=xt[:, blk, :], in0=xt[:, blk, :], scalar=sfin[:, blk:blk + 1],
                                                         in1=fnrow[:, :], op0=ALU.mult, op1=ALU.mult),
                 reads=[xk, "sfin", "fnrow"], writes=[xk])
        if st_tile >= 0:
            S.dma("sp", "st_o%d" % (t % 2), out=out_d[st_tile * T:(st_tile + 1) * T, :].rearrange("(b p) f -> p b f", p=128),
                  in_=xt[:, :, :], reads=[xk], writes=[("outd", st_tile)])

    if wave:
        def load_mw(l_):
            q_ = LS[l_]
            S.dma("sp", "ld_mw0", out=winb[:, :, :].rearrange("p k n -> p (k n)"), in_=q_.win_s, reads=[("mws", l_, 0)], writes=["winb"])
            S.dma("sp", "ld_mw1", out=woutb[:, :, :].rearrange("p k n -> p (k n)"), in_=q_.wout_s, reads=[("mws", l_, 1)], writes=["woutb"])
            S.dma("sp", "ld_mw2", out=wpoolb[:, :, :].rearrange("p g d -> p (g d)"), in_=q_.wpool_s, reads=[("mws", l_, 2)],
                  writes=["wpoolb"])

        load_x(0)
        load_mw(0)
        for t in range(NT):
            xt, xk = xts[t % 2], ("xt", t % 2)
            if t + 1 < NT:
                load_x(t + 1)
            for l_ in range(2):
                setL(l_)
                mixer(t, xt, xk)
                load_mw(1 - l_)
                ffn(t, xt, xk)
            finish(t, xt, xk)
        S.barrier()
        return nc

    setL(0)
    if fused:
        send = [nc.dram_tensor("send%d" % i, [T, D], F32).ap() for i in range(2)]
        gath = [nc.dram_tensor("gath%d" % i, [2 * T, D], F32).ap() for i in range(2)]
        xa = sb("xa", [128, NB, D])
        xg = sb("xg", [128, NB, D])
        coefB = sb("coefB", [128, 64])
        faw = sb("faw", [128, 4])
        for g, w in enumerate(WINDOWS):
            S.op("dve", lambda e: e.tensor_scalar(out=faw[:, g:g + 1], in0=flags[:, 2:3], scalar1=1.0 / w, scalar2=None, op0=ALU.mult),
                 reads=["flags"], writes=[("faw", g)])
            S.op("dve", lambda e: e.tensor_scalar(out=coefB[:, g * 16:(g + 1) * 16], in0=invc[:, g * 16:(g + 1) * 16],
                                                  scalar1=flags[:, 3:4], scalar2=faw[:, g:g + 1], op0=ALU.mult, op1=ALU.add),
                 reads=["invc", "flags", ("faw", g)], writes=[("coefB", g)])
        xt, xk = xts[0], ("xt", 0)
        fl = lambda a: a[:, :, :].rearrange("p b f -> p (b f)")

        def load_xa(i):
            ti = min(i, NT - 1)
            S.dma("sp", "ld_xa", out=xa[:, :, :], in_=x_d[ti * T:(ti + 1) * T, :].rearrange("(b p) f -> p b f", p=128), writes=["xa"])

        load_xa(0)
        for i in range(NT + 1):
            S.op("dve", lambda e: e.tensor_scalar(out=fl(xt), in0=fl(xa), scalar1=flags[:, 2:3], scalar2=None, op0=ALU.mult),
                 reads=["xa", "flags"], writes=[xk])
            if i >= 1:
                gk = ("gath", (i - 1) % 2)
                S.dma("sp", "ld_xg", out=xg[:, :, :], in_=gath[(i - 1) % 2][0:T, :].rearrange("(b p) f -> p b f", p=128),
                      reads=[gk], writes=["xg"])
                S.op("dve", lambda e: e.scalar_tensor_tensor(out=fl(xt), in0=fl(xg), scalar=flags[:, 3:4], in1=fl(xt),
                                                             op0=ALU.mult, op1=ALU.add), reads=["xg", "flags", xk], writes=[xk])
            if i + 1 <= NT:
                load_xa(i + 1)
            mixer(i, xt, xk)
            ffn(i, xt, xk)
            finish(i, xt, xk, st_tile=i - 1)
            if i == 0:
                S.op("dve", lambda e: e.tensor_scalar(out=L.Sst[:, :, :].rearrange("p a b -> p (a b)"),
                                                      in0=L.Sst[:, :, :].rearrange("p a b -> p (a b)"), scalar1=flags[:, 2:3],
                                                      scalar2=None, op0=ALU.mult), reads=[("Sst", L.l), "flags"], writes=[("Sst", L.l)])
                S.op("dve", lambda e: e.tensor_scalar(out=L.ubuf[:, :, 0:16], in0=L.ubuf[:, :, 0:16], scalar1=flags[:, 2:3],
                                                      scalar2=None, op0=ALU.mult), reads=[("ub", L.l, g) for g in range(4)] + ["flags"],
                     writes=[("ub", L.l, g) for g in range(4)])
                S.op("dve", lambda e: e.tensor_scalar(out=L.chalo[:, :, :].rearrange("p a b -> p (a b)"),
                                                      in0=L.chalo[:, :, :].rearrange("p a b -> p (a b)"), scalar1=flags[:, 2:3],
                                                      scalar2=None, op0=ALU.mult), reads=[("chalo", L.l, j) for j in range(44)] + ["flags"],
                     writes=[("chalo", L.l, j) for j in range(44)])
            if i < NT:
                sk_ = ("send", i % 2)
                S.dma("sp", "st_send%d" % (i % 2), out=send[i % 2].rearrange("(b p) f -> p b f", p=128), in_=xt[:, :, :],
                      reads=[xk], writes=[sk_])
                S.custom("pool", "ag%d" % (i % 2),
                         lambda e: e.collective_compute("AllGather", ALU.bypass, replica_groups=groups, ins=[send[i % 2]],
                                                        outs=[gath[i % 2]]),
                         reads=[sk_], writes=[("gath", i % 2)])
        S.barrier()
        return nc

    load_x(0)
    for t in range(NT):
        xt, xk = xts[t % 2], ("xt", t % 2)
        if t + 1 < NT:
            load_x(t + 1)
        if stop == "norm":
            norm_to_hT(xt, xk, 0, 8)
            S.barrier()
            dump("hT", hT[:, :, :], [128, 8, T], BF16)
            dump("rstd", rstd[:, :], [128, 4])
            S.barrier()
            return nc
        try:
            mixer(t, xt, xk)
        except _Stop:
            S.barrier()
            dump("xt", xt[:, :, :], [128, NB, D])
            S.barrier()
            return nc
        if stop == "mixer":
            S.barrier()
            dump("xt", xt[:, :, :], [128, NB, D])
            dump("ymix", ymix[:, :, :], [128, 8, T], BF16)
            dump("oT", oT[:, :, :], [128, 4, T])
            dump("qpp", qpp[:, :, :], [128, 4, T], BF16)
            dump("kpp", kpp[:, :, :], [128, 2, T], BF16)
            dump("gtm", gtm[:, :, :], [128, NB, 256])
            dump("srT", srT[:, :, :], [128, 4, T])
            dump(("Sst", L.l), L.Sst[:, :, :], [128, 2, 256])
            dump("vb", vb[:, :, :], [128, NB, 512], BF16)
            S.barrier()
            return nc
        ffn(t, xt, xk)
        finish(t, xt, xk)
    S.barrier()
    return nc


def _consts():
    tri = np.triu(np.ones((128, 128), np.float32))
    invc = np.zeros((128, 64), np.float32)
    for g, w in enumerate(WINDOWS):
        for t in range(16):
            invc[:, g * 16 + t] = 1.0 / min(t + 1, w)
    hmk = np.zeros((128, 2), np.float32)
    hmk[:64, 0] = 1.0
    hmk[64:, 1] = 1.0
    return {"hmask": hmk, "tri": tri, "maskrep": np.ascontiguousarray(np.tile(tri, (1, 4))), "ident": np.eye(128, dtype=np.float32), "invcnt": invc}


def _core_inputs(xin, b, l, final, P):
    f32 = np.float32
    tp = lambda v, n: np.ascontiguousarray(np.asarray(v, f32).reshape(n, 128).T)
    wg2a = np.zeros((32, 256), f32)
    wg2a[0:16] = P["w_gate2"][l]
    wg2a[16] = P["b_gate"][l]
    ab = np.asarray(P["ada_b"][l], f32)
    m = {
        "x": np.ascontiguousarray(xin, dtype=f32),
        "cT": tp(P["c"][b], 8),
        "ada_w": np.ascontiguousarray(P["ada_w"][l], dtype=f32),
        "ada_bT": tp(ab, 48),
        "ada_bg": np.ascontiguousarray(np.concatenate([ab[2048:3072], ab[5120:6144]])[None, :]),
        "w_in": np.ascontiguousarray(P["w_in"][l], dtype=f32),
        "wg2a": wg2a,
        "w_pool": np.ascontiguousarray(np.transpose(np.asarray(P["w_pool"][l], f32), (1, 0, 2))),
        "pool_scT": tp(P["pool_scale"][l], 4),
        "gla_nT": tp(P["gla_norm"][l], 4),
        "w_out": np.ascontiguousarray(P["w_out"][l], dtype=f32),
        "w_up": np.ascontiguousarray(P["w_up"][l], dtype=f32),
        "conv_wT": np.ascontiguousarray(np.transpose(np.asarray(P["conv_w"][l], f32).reshape(3, 44, 128), (2, 1, 0)).reshape(128, 132)),
        "conv_bT": tp(P["conv_b"][l], 44),
        "w_down": np.ascontiguousarray(P["w_down"][l], dtype=f32),
        "fnrow": np.ascontiguousarray(np.broadcast_to(np.asarray(P["final_norm"], f32)[None, :], (128, D))) if final
        else np.ones((128, D), f32),
        "flags": np.ascontiguousarray(np.broadcast_to(np.array(([1.0, 0.0] if final else [0.0, 1.0]) +
                                                                ([1.0, 0.0] if l == 0 else [0.0, 1.0]), f32)[None, :], (128, 4))),
    }
    m.update(_consts())
    return m


_PER_LAYER = ("ada_w", "ada_bT", "ada_bg", "w_in", "wg2a", "w_pool", "pool_scT", "gla_nT", "w_out", "w_up", "conv_wT", "conv_bT",
              "w_down")


def _core_inputs_wave(xin, b, P):
    m0 = _core_inputs(xin, b, 0, True, P)
    m1 = _core_inputs(xin, b, 1, True, P)
    m = {k: v for k, v in m1.items() if k not in _PER_LAYER}
    for k in _PER_LAYER:
        m[k + "0"] = m0[k]
        m[k + "1"] = m1[k]
    return m


_NC_CACHE = {}


def _get_nc(NT):
    if NT not in _NC_CACHE:
        _NC_CACHE[NT] = build_program(NT)
    return _NC_CACHE[NT]


def run_layer(xs, l, final, P, NT):
    nc = build_program(NT)
    in_maps = [_core_inputs(xs[b], b, l, final, P) for b in range(len(xs))]
    in_maps = in_maps + in_maps
    res = run_bass_kernel_spmd(nc, in_maps, core_ids=list(range(8)))
    return [np.asarray(r["out"]) for r in res.results[:len(xs)]]


def kernel(**inputs):
    P = {k: np.asarray(v) for k, v in inputs.items()}
    x = P["x"]
    NT = SEQ // T
    nc = build_program(NT, wave=True)
    in_maps = [_core_inputs_wave(x[b], b, P) for b in range(BATCH)]
    in_maps = in_maps + in_maps
    res = run_bass_kernel_spmd(nc, in_maps, core_ids=list(range(8)))
    return np.stack([np.asarray(res.results[b]["out"]) for b in range(BATCH)], axis=0).astype(np.float32)
```

```python
import numpy as np
import ml_dtypes
from contextlib import ExitStack
import concourse.bass as bass
import concourse.mybir as mybir
from concourse.bass_utils import run_bass_kernel_spmd

F32 = mybir.dt.float32
BF16 = mybir.dt.bfloat16
AF = mybir.ActivationFunctionType
ALU = mybir.AluOpType

D = 1024
SEQ = 8192
BATCH = 4
DEPTH = 2
T = 256
NB = T // 128
DIN = 2064
DFF = 2816
NPAIR = 22
EPS = 1e-6
GROUPS = (6, 6, 5, 5)
WINDOWS = (2, 4, 8, 16)


class Sched:
    def __init__(self, nc):
        self.nc = nc
        self.e = {"pe": nc.tensor, "act": nc.scalar, "dve": nc.vector, "pool": nc.gpsimd, "sp": nc.sync}
        self.semh = {k: nc.alloc_semaphore(name="s_" + k) for k in self.e}
        self.cnt = {k: 0 for k in self.e}
        self.seen = {k: {} for k in self.e}
        self.lastw = {}
        self.readers = {}

    def _sem(self, sk):
        if sk not in self.semh:
            self.semh[sk] = self.nc.alloc_semaphore(name="d_" + str(len(self.semh)))
            self.cnt[sk] = 0
        return self.semh[sk]

    def _need(self, eng, deps):
        best = {}
        for sk, v in deps:
            if v > best.get(sk, 0):
                best[sk] = v
        for sk, v in best.items():
            if self.seen[eng].get(sk, 0) < v:
                self.e[eng].wait_ge(self.semh[sk], v)
                self.seen[eng][sk] = v

    def _deps(self, reads, writes):
        deps = []
        for k in reads:
            if k in self.lastw:
                deps.append(self.lastw[k])
        for k in writes:
            if k in self.lastw:
                deps.append(self.lastw[k])
            deps.extend(self.readers.get(k, {}).items())
        return deps

    def _commit(self, reads, writes, tag):
        for k in reads:
            r = self.readers.setdefault(k, {})
            if tag[1] > r.get(tag[0], 0):
                r[tag[0]] = tag[1]
        for k in writes:
            self.lastw[k] = tag
            self.readers[k] = {}

    def op(self, eng, fn, reads=(), writes=()):
        self._need(eng, self._deps(reads, writes))
        ins = fn(self.e[eng])
        self.cnt[eng] += 1
        ins.then_inc(self.semh[eng], 1)
        self._commit(reads, writes, (eng, self.cnt[eng]))

    def dma(self, eng, sk, out, in_, reads=(), writes=(), nodep_writes=()):
        self._sem(sk)
        self._need(eng, self._deps(reads, writes))
        ins = self.e[eng].dma_start(out=out, in_=in_)
        self.cnt[sk] += 16
        ins.then_inc(self.semh[sk], 16)
        self._commit(reads, list(writes) + list(nodep_writes), (sk, self.cnt[sk]))

    def custom(self, eng, sk, fn, reads=(), writes=()):
        self._sem(sk)
        self._need(eng, self._deps(reads, writes))
        ins = fn(self.e[eng])
        self.cnt[sk] += 16
        ins.then_inc(self.semh[sk], 16)
        self._commit(reads, writes, (sk, self.cnt[sk]))

    def barrier(self, engines=None):
        for eng in engines or self.e:
            self._need(eng, [(sk, c) for sk, c in self.cnt.items() if c > 0])


def build_program(NT, stop=None, fused=False, groups=None, wave=False):
    S_tok = NT * T
    nc = bass.Bass("TRN2", target_bir_lowering=False)

    def din(name, shape, dt=F32):
        return nc.dram_tensor(name, list(shape), dt, kind="ExternalInput").ap()

    from types import SimpleNamespace
    NL = 2 if wave else 1
    x_d = din("x", [S_tok, D])
    cT_d = din("cT", [128, 8])
    LS = []
    for l_ in range(NL):
        sf = str(l_) if wave else ""
        LS.append(SimpleNamespace(
            l=l_,
            adaw_d=din("ada_w" + sf, [D, 6 * D]), adabT_d=din("ada_bT" + sf, [128, 48]), adabg_d=din("ada_bg" + sf, [1, 2048]),
            win_d=din("w_in" + sf, [D, DIN]), wg2a_d=din("wg2a" + sf, [32, 256]), wpool_d=din("w_pool" + sf, [128, 4, 128]),
            psc_d=din("pool_scT" + sf, [128, 4]), gln_d=din("gla_nT" + sf, [128, 4]), wout_d=din("w_out" + sf, [D, D]),
            wup_d=din("w_up" + sf, [D, 2 * DFF]), cw_d=din("conv_wT" + sf, [128, 132]), cb_d=din("conv_bT" + sf, [128, 44]),
            wdn_d=din("w_down" + sf, [DFF, D])))
    L = SimpleNamespace()

    def setL(l_):
        L.__dict__.update(LS[l_].__dict__)
    fn_d = din("fnrow", [128, D])
    flag_d = din("flags", [128, 4])
    tri_d = din("tri", [128, 128])
    mask_d = din("maskrep", [128, 512])
    ident_d = din("ident", [128, 128])
    invc_d = din("invcnt", [128, 64])
    hm_d = din("hmask", [128, 2])
    out_d = nc.dram_tensor("out", [S_tok, D], F32, kind="ExternalOutput").ap()
    for l_ in range(NL):
        LS[l_].wup_s = nc.dram_tensor("wup_s%d" % l_, [NPAIR, 128, 8, 256], BF16).ap()
        LS[l_].wdn_s = nc.dram_tensor("wdn_s%d" % l_, [NPAIR, 128, 1024], BF16).ap()
        LS[l_].win_s = nc.dram_tensor("win_s%d" % l_, [128, 8 * DIN], BF16).ap()
        LS[l_].wout_s = nc.dram_tensor("wout_s%d" % l_, [128, 8 * D], BF16).ap()
        LS[l_].wpool_s = nc.dram_tensor("wpool_s%d" % l_, [128, 512], BF16).ap()

    S = Sched(nc)
    dbg = {}

    def dump(name, ap, shape, dt=F32, key=None):
        d = nc.dram_tensor("dbg_" + name, list(shape), dt, kind="ExternalOutput").ap()
        S.dma("sp", "dbg_" + name, out=d, in_=ap, reads=[key] if key is not None else [])

    class _Stop(Exception):
        pass
    sb = lambda name, shape, dt=F32: nc.alloc_sbuf_tensor("sb_" + name, list(shape), dt)

    winb = sb("winb", [128, 8, DIN], BF16)
    woutb = sb("woutb", [128, 8, D], BF16)
    wpoolb = sb("wpoolb", [128, 4, 128], BF16)
    for l_ in range(NL):
        q_ = LS[l_]
        n_ = lambda nm: nm + str(l_)
        q_.modT = sb(n_("modT"), [128, 48])
        q_.pscT = sb(n_("pscT"), [128, 4])
        q_.glnT = sb(n_("glnT"), [128, 4])
        q_.cwT = sb(n_("cwT"), [128, 132])
        q_.cbT = sb(n_("cbT"), [128, 44])
        q_.wg2a = sb(n_("wg2a"), [32, 256])
        q_.chalo = sb(n_("chalo"), [128, 44, 2])
        q_.ubuf = sb(n_("ubuf"), [128, 4, 16 + T])
        q_.Sst = sb(n_("Sst"), [128, 2, 256])
    fnrow = sb("fnrow", [128, D])
    flags = sb("flags", [128, 4])
    tri = sb("tri", [128, 128])
    maskrep = sb("maskrep", [128, 512])
    identf = sb("identf", [128, 128])
    identb = sb("identb", [128, 128], BF16)
    invc = sb("invc", [128, 64])
    hm = sb("hm", [128, 2])
    onesD = sb("onesD", [128, 128])
    epsT = sb("epsT", [128, 1])
    zaug = sb("zaug", [32, T])

    pf = [nc.alloc_psum_tensor("pf%d" % i, [128, 512], F32) for i in range(6)]
    pt = [nc.alloc_psum_tensor("pt%d" % i, [128, 1024], BF16) for i in range(2)]
    rr = {"pf": 0, "pt": 0}

    def next_pf():
        i = rr["pf"]
        rr["pf"] = (i + 1) % len(pf)
        return pf[i], ("pf", i)

    def next_pt():
        i = rr["pt"]
        rr["pt"] = (i + 1) % len(pt)
        return pt[i], ("pt", i)

    def load_const(dst, src, key):
        S.dma("sp", "ld_" + key, out=dst, in_=src, writes=[key])

    load_const(fnrow[:, :], fn_d, "fnrow")
    load_const(flags[:, :], flag_d, "flags")
    load_const(tri[:, :], tri_d, "tri")
    load_const(maskrep[:, :], mask_d, "maskrep")
    load_const(identf[:, :], ident_d, "identf")
    load_const(invc[:, :], invc_d, "invc")
    load_const(hm[:, :], hm_d, "hm")
    S.op("dve", lambda e: e.tensor_copy(identb[:, :], identf[:, :]), reads=["identf"], writes=["identb"])
    S.op("dve", lambda e: e.memset(onesD[:, :], 1.0 / 128.0), writes=["onesD"])
    S.op("dve", lambda e: e.memset(epsT[:, :], EPS), writes=["epsT"])
    for l_ in range(NL):
        setL(l_)
        S.dma("sp", "ld_pscT%d" % l_, out=L.pscT[:, :], in_=L.psc_d, writes=[("pscT", L.l)])
        S.dma("sp", "ld_glnT%d" % l_, out=L.glnT[:, :], in_=L.gln_d, writes=[("glnT", L.l)])
        S.dma("sp", "ld_cwT%d" % l_, out=L.cwT[:, :], in_=L.cw_d, writes=[("cwT", L.l)])
        S.dma("sp", "ld_cbT%d" % l_, out=L.cbT[:, :], in_=L.cb_d, writes=[("cbT", L.l)])
        S.dma("sp", "ld_wg2a%d" % l_, out=L.wg2a[:, :], in_=L.wg2a_d, writes=[("wg2a", L.l)])
        S.op("dve", lambda e: e.memset(L.chalo[:, :, :], 0.0), writes=[("chalo", L.l, j) for j in range(44)])
        S.op("dve", lambda e: e.memset(L.ubuf[:, :, :], 0.0), writes=[("ub", L.l, g) for g in range(4)])
        S.op("dve", lambda e: e.memset(L.Sst[:, :, :], 0.0), writes=[("Sst", L.l)])
    S.op("dve", lambda e: e.memset(zaug[:, :], 1.0), writes=["zaug"])

    with ExitStack() as ps_:
        pb = lambda name, shape, dt=F32: ps_.enter_context(nc.sbuf_tensor("pb_" + name, list(shape), dt))
        cT = pb("cT", [128, 8])
        cact = pb("cact", [128, 8])
        ones128 = pb("ones128", [128, 128])
        crep = pb("crep", [128, 8, 128])
        adabT = pb("adabT", [128, 48])
        adabg = pb("adabg", [1, 2048])
        Gt = [pb("G1t", [128, D]), pb("G2t", [128, D])]
        ablk = [pb("ablk%d" % i, [128, 8, 512]) for i in range(2)]
        stg = [pb("stg%d" % i, [128, 2 * DFF]) for i in range(2)]
        cvt = [pb("cvt%d" % i, [128, 2 * DFF], BF16) for i in range(2)]

        load_const(cT[:, :], cT_d, "cT")
        S.op("act", lambda e: e.activation(out=cact[:, :], in_=cT[:, :], func=AF.Silu), reads=["cT"], writes=["cact"])
        S.op("dve", lambda e: e.memset(ones128[:, :], 1.0), writes=["ones128"])
        for k in range(8):
            S.op("dve", lambda e: e.tensor_scalar(out=crep[:, k, :], in0=ones128[:, :], scalar1=cact[:, k:k + 1],
                                                  scalar2=None, op0=ALU.mult),
                 reads=["ones128", "cact"], writes=[("crep", k)])
        for l_ in range(NL):
            setL(l_)
            S.dma("sp", "ld_adabT", out=adabT[:, :], in_=L.adabT_d, writes=["adabT"])
            S.dma("sp", "ld_adabg", out=adabg[:, :], in_=L.adabg_d, writes=["adabg"])
            for nb in range(12):
                a = ablk[nb % 2]
                ak = ("ablk", nb % 2)
                S.dma("sp", "ld_ablk%d" % (nb % 2), out=a[:, :, :],
                      in_=L.adaw_d[:, nb * 512:(nb + 1) * 512].rearrange("(k p) n -> p k n", p=128), writes=[ak])
                ps, pk = next_pf()
                if nb in (4, 5, 10, 11):
                    gi = 0 if nb < 6 else 1
                    half = nb % 2
                    boff = (0 if nb < 6 else 1024) + half * 512

                    def f(e):
                        for k in range(8):
                            e.matmul(ps[:, :], lhsT=crep[:, k, :], rhs=a[:, k, :], start=(k == 0), stop=False)
                        return e.matmul(ps[:, :], lhsT=ones128[0:1, :], rhs=adabg[0:1, boff:boff + 512], start=False, stop=True)
                    S.op("pe", f, reads=[ak, "ones128", "adabg"] + [("crep", k) for k in range(8)], writes=[pk])
                    S.op("act", lambda e: e.activation(out=Gt[gi][:, half * 512:(half + 1) * 512], in_=ps[:, :], func=AF.Copy),
                         reads=[pk], writes=[("Gt", gi, half)])
                else:
                    def f(e):
                        for m in range(4):
                            for k in range(8):
                                ins = e.matmul(ps[:, m:m + 1], lhsT=a[:, k, m * 128:(m + 1) * 128], rhs=cact[:, k:k + 1],
                                               start=(k == 0), stop=(k == 7))
                        return ins
                    S.op("pe", f, reads=[ak, "cact"], writes=[pk])
                    S.op("dve", lambda e: e.tensor_tensor(out=L.modT[:, 4 * nb:4 * nb + 4], in0=ps[:, 0:4],
                                                          in1=adabT[:, 4 * nb:4 * nb + 4], op=ALU.add),
                         reads=[pk, "adabT"], writes=[("modT", L.l, nb)])
            for nb in (2, 3, 8, 9):
                S.op("dve", lambda e: e.tensor_scalar(out=L.modT[:, 4 * nb:4 * nb + 4], in0=L.modT[:, 4 * nb:4 * nb + 4],
                                                      scalar1=1.0, scalar2=None, op0=ALU.add),
                     reads=[("modT", L.l, nb)], writes=[("modT", L.l, nb)])
            ci = [0]

            def stage_load(src, ncols):
                i = ci[0] % 2
                ci[0] += 1
                S.dma("sp", "ld_stg%d" % i, out=stg[i][:, 0:ncols], in_=src, writes=[("stg", i)])
                return i

            for k in range(8):
                i = stage_load(L.win_d[k * 128:(k + 1) * 128, :], DIN)
                eng = "act" if k % 2 == 0 else "dve"
                if eng == "act":
                    S.op("act", lambda e: e.activation(out=winb[:, k, :], in_=stg[i][:, 0:DIN], func=AF.Copy),
                         reads=[("stg", i)], writes=["winb"])
                else:
                    S.op("dve", lambda e: e.tensor_copy(winb[:, k, :], stg[i][:, 0:DIN]), reads=[("stg", i)], writes=["winb"])
            for k in range(8):
                i = stage_load(L.wout_d[k * 128:(k + 1) * 128, :], D)
                S.op("dve", lambda e: e.tensor_tensor(out=woutb[:, k, :], in0=stg[i][:, 0:D], in1=Gt[0][:, :], op=ALU.mult),
                     reads=[("stg", i), ("Gt", 0, 0), ("Gt", 0, 1)], writes=["woutb"])
            i = stage_load(L.wpool_d.rearrange("p g d -> p (g d)"), 512)
            S.op("dve", lambda e: e.tensor_copy(wpoolb[:, :, :].rearrange("p g d -> p (g d)"), stg[i][:, 0:512]),
                 reads=[("stg", i)], writes=["wpoolb"])
            for k in range(8):
                i = stage_load(L.wup_d[k * 128:(k + 1) * 128, :], 2 * DFF)
                c3 = cvt[i][:, :].rearrange("p (i c) -> p i c", c=256)
                S.op("act", lambda e: e.activation(out=c3[:, :, 0:128], in_=stg[i][:, 0:DFF].rearrange("p (i c) -> p i c", c=128),
                                                   func=AF.Copy), reads=[("stg", i)], writes=[("cvt", i, 0)])
                S.op("dve", lambda e: e.tensor_copy(c3[:, :, 128:256], stg[i][:, DFF:2 * DFF].rearrange("p (i c) -> p i c", c=128)),
                     reads=[("stg", i)], writes=[("cvt", i, 1)])
                for hh in range(2):
                    S.dma("sp", "st_scr%d" % i, out=L.wup_s[hh * 11:(hh + 1) * 11, :, k, :].rearrange("i p c -> p i c"),
                          in_=c3[:, hh * 11:(hh + 1) * 11, :], reads=[("cvt", i, 0), ("cvt", i, 1)], nodep_writes=["scratch%d" % i])
            for j in range(NPAIR):
                i = stage_load(L.wdn_d[j * 128:(j + 1) * 128, :], D)
                S.op("dve", lambda e: e.tensor_tensor(out=cvt[i][:, 0:D], in0=stg[i][:, 0:D], in1=Gt[1][:, :], op=ALU.mult),
                     reads=[("stg", i), ("Gt", 1, 0), ("Gt", 1, 1)], writes=[("cvt", i, 0), ("cvt", i, 1)])
                S.dma("sp", "st_scr%d" % i, out=L.wdn_s[j], in_=cvt[i][:, 0:D],
                      reads=[("cvt", i, 0), ("cvt", i, 1)], nodep_writes=["scratch%d" % i])
            if wave:
                S.dma("sp", "st_mw0", out=L.win_s, in_=winb[:, :, :].rearrange("p k n -> p (k n)"), reads=["winb"],
                      nodep_writes=[("mws", L.l, 0)])
                S.dma("sp", "st_mw1", out=L.wout_s, in_=woutb[:, :, :].rearrange("p k n -> p (k n)"), reads=["woutb"],
                      nodep_writes=[("mws", L.l, 1)])
                S.dma("sp", "st_mw2", out=L.wpool_s, in_=wpoolb[:, :, :].rearrange("p g d -> p (g d)"), reads=["wpoolb"],
                      nodep_writes=[("mws", L.l, 2)])
        S.barrier()
        if stop == "prep":
            S.barrier()
            dump("winb", winb[:, :, :], [128, 8, DIN], BF16)
            dump("woutb", woutb[:, :, :], [128, 8, D], BF16)
            dump("modT", L.modT[:, :], [128, 48])
            dump("G1t", Gt[0][:, :], [128, D])
            S.barrier()
            return nc

    xts = [sb("xt%d" % i, [128, NB, D]) for i in range(1 if fused else 2)]
    xnb = sb("xnb", [128, NB, D], BF16)
    junk = sb("junk", [128, D], BF16)
    hT = sb("hT", [128, 8, T], BF16)
    ss = sb("ss", [128, 4])
    rstd = sb("rstd", [128, 4])
    sfin = sb("sfin", [128, 4])
    wups = [sb("wups%d" % i, [128, 8, 256], BF16) for i in range(3)]
    wdng = [sb("wdng%d" % i, [128, 6, D], BF16) for i in range(2)]
    hidg = [sb("hidg%d" % i, [128, 6, T], BF16) for i in range(2)]
    pA = sb("pA", [128, 16 + T])
    pB = sb("pB", [128, 16 + T])
    tmp16 = sb("tmp16", [128, 16])
    dT = sb("dT", [128, 4, T], BF16)
    qT = sb("qT", [128, 2, T])
    kT = sb("kT", [128, 2, T])
    srT = sb("srT", [128, 4, T])
    vb = sb("vb", [128, NB, 512], BF16)
    gtm = sb("gtm", [128, NB, 256])
    e1 = sb("e1", [128, 2, T])
    e2 = sb("e2", [128, 2, T])
    pbias = sb("pbias", [128, 2 * NB])
    nbias = sb("nbias", [128, 2 * NB])
    em = sb("em", [128, 2 * NB])
    dl = sb("dl", [128, 2 * NB])
    qpp = sb("qpp", [128, 4, T], BF16)
    kpp = sb("kpp", [128, 2, T], BF16)
    kptok = sb("kptok", [128, NB, 256], BF16)
    am = sb("am", [128, 512], BF16)
    Smb = sb("Smb", [128, 2, 256], BF16)
    tmpU = sb("tmpU", [128, 2, 256])
    oT = sb("oT", [128, 4, T])
    osq = sb("osq", [128, 4, T])
    rso = sb("rso", [128, 4, T])
    otmp = osq
    ymix = sb("ymix", [128, 8, T], BF16)
    upb_ = [sb("up%d" % i, [128, 2 + T]) for i in range(4)]
    tcv = [sb("tcv%d" % i, [128, T]) for i in range(4)]
    sact = [sb("sact%d" % i, [128, T]) for i in range(2)]

    def load_x(t):
        S.dma("sp", "ld_x%d" % (t % 2), out=xts[t % 2][:, :, :],
              in_=x_d[t * T:(t + 1) * T, :].rearrange("(b p) f -> p b f", p=128), writes=[("xt", t % 2)])

    def norm_stats(xt, xk):
        S.op("dve", lambda e: e.memset(ss[:, :], 0.0), writes=["ss"])
        for blk in range(NB):
            S.op("act", lambda e: e.activation(out=junk[:, :], in_=xt[:, blk, :], func=AF.Square,
                                               accum_out=ss[:, blk:blk + 1]), reads=[xk], writes=["junk", "ss"])
        S.op("act", lambda e: e.activation(out=rstd[:, 0:NB], in_=ss[:, 0:NB], func=AF.Sqrt, scale=1.0 / D, bias=epsT[:, 0:1]),
             reads=["ss", "epsT"], writes=["rstd"])
        S.op("dve", lambda e: e.reciprocal(rstd[:, 0:NB], rstd[:, 0:NB]), reads=["rstd"], writes=["rstd"])

    def norm_to_hT(xt, xk, sh_off, sc_off):
        norm_stats(xt, xk)
        for blk in range(NB):
            S.op("dve", lambda e: e.tensor_scalar(out=xnb[:, blk, :], in0=xt[:, blk, :], scalar1=rstd[:, blk:blk + 1],
                                                  scalar2=None, op0=ALU.mult),
                 reads=[xk, "rstd"], writes=[("xnb", blk)])
        for k in range(8):
            p, pk = next_pt()

            def f(e):
                for blk in range(NB):
                    ins = e.transpose(out=p[:, blk * 128:(blk + 1) * 128], in_=xnb[:, blk, k * 128:(k + 1) * 128],
                                      identity=identb[:, :])
                return ins
            S.op("pe", f, reads=[("xnb", b) for b in range(NB)] + ["identb"], writes=[pk])
            S.op("act", lambda e: e.activation(out=hT[:, k, :], in_=p[:, 0:T], func=AF.Identity,
                                               scale=L.modT[:, sc_off + k:sc_off + k + 1],
                                               bias=L.modT[:, sh_off + k:sh_off + k + 1]),
                 reads=[pk] + [("modT", L.l, i) for i in range(12)], writes=[("hT", k)])

    hT_all = [("hT", k) for k in range(8)]

    def proj_fm(col, M):
        ps, pk = next_pf()

        def f(e):
            for k in range(8):
                ins = e.matmul(ps[0:M, 0:T], lhsT=winb[:, k, col:col + M], rhs=hT[:, k, :], start=(k == 0), stop=(k == 7))
            return ins
        S.op("pe", f, reads=hT_all + ["winb"], writes=[pk])
        return ps, pk

    def mixer(t, xt, xk):
        norm_to_hT(xt, xk, 0, 8)
        for g in range(4):
            ps, pk = proj_fm(g * 128, 128)
            S.op("dve", lambda e: e.tensor_copy(L.ubuf[:, g, 16:16 + T], ps[:, 0:T]), reads=[pk], writes=[("ub", L.l, g)])
        for c in range(2):
            ps, pk = proj_fm(512 + c * 128, 128)
            S.op("dve", lambda e: e.tensor_scalar(out=qT[:, c, :], in0=ps[:, 0:T], scalar1=0.125, scalar2=None, op0=ALU.mult),
                 reads=[pk], writes=[("qT", c)])
        for c in range(2):
            ps, pk = proj_fm(768 + c * 128, 128)
            S.op("dve", lambda e: e.tensor_copy(kT[:, c, :], ps[:, 0:T]), reads=[pk], writes=[("kT", c)])
        for c in range(4):
            ps, pk = proj_fm(1536 + c * 128, 128)
            S.op("act", lambda e: e.activation(out=srT[:, c, :], in_=ps[:, 0:T], func=AF.Silu), reads=[pk], writes=[("srT", c)])
        ps, pk = proj_fm(2048, 16)
        S.op("dve", lambda e: e.tensor_copy(zaug[0:16, :], ps[0:16, 0:T]), reads=[pk], writes=["zaug"])
        for blk in range(NB):
            ps, pk = next_pf()

            def f(e):
                for k in range(8):
                    ins = e.matmul(ps[:, :], lhsT=hT[:, k, blk * 128:(blk + 1) * 128], rhs=winb[:, k, 1024:1536],
                                   start=(k == 0), stop=(k == 7))
                return ins
            S.op("pe", f, reads=hT_all + ["winb"], writes=[pk])
            S.op("act", lambda e: e.activation(out=vb[:, blk, :], in_=ps[:, :], func=AF.Copy), reads=[pk], writes=[("vb", blk)])

        if stop == "inproj":
            raise _Stop()
        W = 16 + T
        for g, w in enumerate(WINDOWS):
            ub = L.ubuf[:, g, :]
            uk = ("ub", L.l, g)
            add = lambda o, a, b_: (lambda e: e.tensor_tensor(out=o, in0=a, in1=b_, op=ALU.add))
            S.op("pool", add(pA[:, 1:W], ub[:, 1:W], ub[:, 0:W - 1]), reads=[uk], writes=["pA"])
            cur, ck = pA, "pA"
            if w >= 4:
                S.op("pool", add(pB[:, 3:W], pA[:, 3:W], pA[:, 1:W - 2]), reads=["pA"], writes=["pB"])
                cur, ck = pB, "pB"
            if w >= 8:
                S.op("pool", add(pA[:, 7:W], pB[:, 7:W], pB[:, 3:W - 4]), reads=["pB"], writes=["pA"])
                cur, ck = pA, "pA"
            if w >= 16:
                S.op("pool", add(pB[:, 15:W], pA[:, 15:W], pA[:, 7:W - 8]), reads=["pA"], writes=["pB"])
                cur, ck = pB, "pB"
            S.op("dve", lambda e: e.scalar_tensor_tensor(out=dT[:, g, :], in0=cur[:, 16:W], scalar=1.0 / w, in1=ub[:, 16:W],
                                                         op0=ALU.mult, op1=ALU.subtract), reads=[ck, uk], writes=[("dT", g)])
            if t == 0 or (fused and t == 1):
                cf, cfk = (invc, "invc") if t == 0 else (coefB, ("coefB", g))
                S.op("pool", lambda e: e.tensor_tensor(out=tmp16[:, :], in0=cur[:, 16:32], in1=cf[:, g * 16:(g + 1) * 16],
                                                       op=ALU.mult), reads=[ck, cfk], writes=["tmp16"])
                S.op("pool", lambda e: e.tensor_tensor(out=dT[:, g, 0:16], in0=tmp16[:, :], in1=ub[:, 16:32], op=ALU.subtract),
                     reads=["tmp16", uk], writes=[("dT", g)])
            S.op("pool", lambda e: e.tensor_copy(ub[:, 0:16], ub[:, T:T + 16]), reads=[uk], writes=[uk])
            ps, pk = next_pf()
            S.op("pe", lambda e: e.matmul(ps[:, 0:T], lhsT=wpoolb[:, g, :], rhs=dT[:, g, :], start=True, stop=True),
                 reads=[("dT", g), "wpoolb"], writes=[pk])
            S.op("act", lambda e: e.activation(out=ymix[:, g, :], in_=ps[:, 0:T], func=AF.Identity, scale=L.pscT[:, g:g + 1]),
                 reads=[pk, ("pscT", L.l)], writes=[("ymix", g)])

        if stop == "pool":
            raise _Stop()
        ps, pk = next_pf()

        def f(e):
            for blk in range(NB):
                ins = e.matmul(ps[:, blk * 256:(blk + 1) * 256], lhsT=zaug[0:32, blk * 128:(blk + 1) * 128], rhs=L.wg2a[0:32, :],
                               start=True, stop=True)
            return ins
        S.op("pe", f, reads=["zaug", ("wg2a", L.l)], writes=[pk])
        gflat = gtm[:, :, :].rearrange("p b c -> p (b c)")
        S.op("act", lambda e: e.activation(out=gflat, in_=ps[:, 0:NB * 256], func=AF.Exp, scale=-1.0), reads=[pk], writes=["gtm"])
        S.op("act", lambda e: e.activation(out=gflat, in_=gflat, func=AF.Ln, bias=1.0), reads=["gtm"], writes=["gtm"])
        if stop == "gate":
            raise _Stop()
        cps, ck_ = next_pf()

        def f(e):
            for fc in range(2):
                for blk in range(NB):
                    ins = e.matmul(cps[:, fc * T + blk * 128:fc * T + (blk + 1) * 128],
                                   lhsT=gtm[:, blk, fc * 128:(fc + 1) * 128], rhs=tri[:, :], start=True, stop=True)
            return ins
        S.op("pe", f, reads=["gtm", "tri"], writes=[ck_])
        c4 = cps[:, :].rearrange("p (a c) -> p a c", c=128)
        S.op("dve", lambda e: e.tensor_scalar(out=pbias[:, :], in0=c4[:, :, 63], scalar1=1.0 / 16, scalar2=None, op0=ALU.mult),
             reads=[ck_], writes=["pbias"])
        S.op("dve", lambda e: e.tensor_scalar(out=nbias[:, :], in0=c4[:, :, 63], scalar1=-1.0 / 16, scalar2=None, op0=ALU.mult),
             reads=[ck_], writes=["nbias"])
        S.op("act", lambda e: e.activation(out=em[:, :], in_=c4[:, :, 63], func=AF.Exp, scale=-1.0 / 16), reads=[ck_], writes=["em"])
        S.op("act", lambda e: e.activation(out=dl[:, :], in_=c4[:, :, 127], func=AF.Exp, scale=-1.0 / 16), reads=[ck_], writes=["dl"])
        for fc in range(2):
            for blk in range(NB):
                a = fc * NB + blk
                cs = slice(blk * 128, (blk + 1) * 128)
                S.op("act", lambda e: e.activation(out=e1[:, fc, cs], in_=c4[:, a, :], func=AF.Exp, scale=-1.0 / 16,
                                                   bias=pbias[:, a:a + 1]), reads=[ck_, "pbias"], writes=[("e1", fc)])
                S.op("act", lambda e: e.activation(out=e2[:, fc, cs], in_=c4[:, a, :], func=AF.Exp, scale=1.0 / 16,
                                                   bias=nbias[:, a:a + 1]), reads=[ck_, "nbias"], writes=[("e2", fc)])
            for hl in range(2):
                S.op("dve", lambda e: e.scalar_tensor_tensor(out=qpp[:, 2 * fc + hl, :], in0=qT[:, fc, :], scalar=hm[:, hl:hl + 1],
                                                             in1=e1[:, fc, :], op0=ALU.mult, op1=ALU.mult),
                     reads=[("qT", fc), ("e1", fc), "hm"], writes=[("qpp", 2 * fc + hl)])
            S.op("dve", lambda e: e.tensor_tensor(out=kpp[:, fc, :], in0=kT[:, fc, :], in1=e2[:, fc, :], op=ALU.mult),
                 reads=[("kT", fc), ("e2", fc)], writes=[("kpp", fc)])
        if stop == "cum":
            raise _Stop()
        p, pk = next_pt()

        def f(e):
            for blk in range(NB):
                for fc in range(2):
                    ins = e.transpose(out=p[:, (blk * 2 + fc) * 128:(blk * 2 + fc + 1) * 128],
                                      in_=kpp[:, fc, blk * 128:(blk + 1) * 128], identity=identb[:, :])
            return ins
        S.op("pe", f, reads=[("kpp", 0), ("kpp", 1), "identb"], writes=[pk])
        S.op("dve", lambda e: e.tensor_copy(kptok[:, :, :].rearrange("p b c -> p (b c)"), p[:, 0:NB * 256]),
             reads=[pk], writes=["kptok"])
        if stop == "ktok":
            raise _Stop()
        for blk in range(NB):
            cs = slice(blk * 128, (blk + 1) * 128)
            for fc in range(2):
                a = fc * NB + blk
                S.op("dve", lambda e: e.tensor_scalar(out=Smb[:, fc, :], in0=L.Sst[:, fc, :], scalar1=em[:, a:a + 1], scalar2=None,
                                                      op0=ALU.mult), reads=[("Sst", L.l), "em"], writes=[("Smb", fc)])
            aps, ak = next_pf()

            def f(e):
                for h in range(4):
                    fc, hl = h // 2, h % 2
                    rs = slice(64 * hl, 64 * hl + 64)
                    ins = e.matmul(aps[:, h * 128:(h + 1) * 128], lhsT=kpp[:, fc, cs], rhs=qpp[:, h, cs], start=True, stop=True)
                return ins
            S.op("pe", f, reads=[("kpp", 0), ("kpp", 1)] + [("qpp", h) for h in range(4)], writes=[ak])
            S.op("dve", lambda e: e.tensor_tensor(out=am[:, :], in0=aps[:, :], in1=maskrep[:, :], op=ALU.mult),
                 reads=[ak, "maskrep"], writes=["am"])
            ops_, ok = next_pf()

            def f(e):
                for h in range(4):
                    fc, hl = h // 2, h % 2
                    rs = slice(64 * hl, 64 * hl + 64)
                    e.matmul(ops_[:, h * 128:(h + 1) * 128], lhsT=vb[:, blk, h * 128:(h + 1) * 128], rhs=am[:, h * 128:(h + 1) * 128],
                             start=True, stop=False)
                    ins = e.matmul(ops_[:, h * 128:(h + 1) * 128], lhsT=Smb[:, fc, hl * 128:(hl + 1) * 128], rhs=qpp[:, h, cs],
                                   start=False, stop=True)
                return ins
            S.op("pe", f, reads=[("vb", blk), "am", ("Smb", 0), ("Smb", 1)] + [("qpp", h) for h in range(4)], writes=[ok])
            S.op("act", lambda e: e.activation(out=oT[:, :, cs], in_=ops_[:, :].rearrange("p (h c) -> p h c", c=128), func=AF.Copy),
                 reads=[ok], writes=["oT"])
            ups, uk_ = next_pf()

            def f(e):
                for fc in range(2):
                    ins = e.matmul(ups[:, fc * 256:(fc + 1) * 256], lhsT=kptok[:, blk, fc * 128:(fc + 1) * 128],
                                   rhs=vb[:, blk, fc * 256:(fc + 1) * 256], start=True, stop=True)
                return ins
            S.op("pe", f, reads=["kptok", ("vb", blk)], writes=[uk_])
            for fc in range(2):
                a = fc * NB + blk
                S.op("dve", lambda e: e.tensor_scalar(out=tmpU[:, fc, :], in0=ups[:, fc * 256:(fc + 1) * 256],
                                                      scalar1=e1[:, fc, blk * 128 + 127:blk * 128 + 128], scalar2=None, op0=ALU.mult),
                     reads=[uk_, ("e1", fc)], writes=[("tmpU", fc)])
                S.op("dve", lambda e: e.scalar_tensor_tensor(out=L.Sst[:, fc, :], in0=L.Sst[:, fc, :], scalar=dl[:, a:a + 1],
                                                             in1=tmpU[:, fc, :], op0=ALU.mult, op1=ALU.add),
                     reads=[("Sst", L.l), "dl", ("tmpU", fc)], writes=[("Sst", L.l)])
        if stop == "rec":
            raise _Stop()
        S.op("act", lambda e: e.activation(out=osq[:, :, :].rearrange("p h t -> p (h t)"),
                                           in_=oT[:, :, :].rearrange("p h t -> p (h t)"), func=AF.Square), reads=["oT"], writes=["osq"])
        for hp in range(2):
            ps, pk = next_pf()

            def f(e):
                for j in range(2):
                    ins = e.matmul(ps[:, j * T:(j + 1) * T], lhsT=onesD[:, :], rhs=osq[:, 2 * hp + j, :], start=True, stop=True)
                return ins
            S.op("pe", f, reads=["osq", "onesD"], writes=[pk])
            rv = rso[:, 2 * hp:2 * hp + 2, :].rearrange("p h t -> p (h t)")
            S.op("act", lambda e: e.activation(out=rv, in_=ps[:, 0:2 * T], func=AF.Sqrt, bias=epsT[:, 0:1]), reads=[pk, "epsT"],
                 writes=[("rso", hp)])
            S.op("dve", lambda e: e.reciprocal(rv, rv), reads=[("rso", hp)], writes=[("rso", hp)])
        for h in range(4):
            S.op("dve", lambda e: e.scalar_tensor_tensor(out=otmp[:, h, :], in0=oT[:, h, :], scalar=L.glnT[:, h:h + 1], in1=rso[:, h, :],
                                                         op0=ALU.mult, op1=ALU.mult), reads=["oT", ("glnT", L.l), ("rso", h // 2)],
                 writes=["osq"])
            S.op("pool", lambda e: e.tensor_tensor(out=ymix[:, 4 + h, :], in0=otmp[:, h, :], in1=srT[:, h, :], op=ALU.mult),
                 reads=["osq", ("srT", h)], writes=[("ymix", 4 + h)])
        if stop == "onorm":
            raise _Stop()
        for blk in range(NB):
            for half in range(2):
                ps, pk = next_pf()

                def f(e):
                    for kc in range(8):
                        ins = e.matmul(ps[:, :], lhsT=ymix[:, kc, blk * 128:(blk + 1) * 128], rhs=woutb[:, kc, half * 512:(half + 1) * 512],
                                       start=(kc == 0), stop=(kc == 7))
                    return ins
                S.op("pe", f, reads=[("ymix", i) for i in range(8)] + ["woutb"], writes=[pk])
                S.op("dve", lambda e: e.tensor_tensor(out=xt[:, blk, half * 512:(half + 1) * 512], in0=xt[:, blk, half * 512:(half + 1) * 512],
                                                      in1=ps[:, :], op=ALU.add), reads=[pk, xk], writes=[xk])

    wctr = {"up": 0, "dn": 0}

    def ffn(t, xt, xk):
        norm_to_hT(xt, xk, 24, 32)
        pend = []

        def issue_up(i):
            s = wctr["up"] % 3
            wctr["up"] += 1
            S.dma("sp", "ld_wup%d" % s, out=wups[s][:, :, :], in_=L.wup_s[i], reads=["scratch0", "scratch1"], writes=[("wups", s)])
            return s

        def issue_dn(i0, gsz):
            s = wctr["dn"] % 2
            wctr["dn"] += 1
            S.dma("sp", "ld_wdn%d" % s, out=wdng[s][:, 0:gsz, :], in_=L.wdn_s[i0:i0 + gsz].rearrange("i p n -> p i n"),
                  reads=["scratch0", "scratch1"], writes=[("wdng", s)])
            return s

        slots = {}
        slots[0] = issue_up(0)
        slots[1] = issue_up(1)
        i0 = 0
        for gi, gsz in enumerate(GROUPS):
            ds = issue_dn(i0, gsz)
            hs = gi % 2
            for jj in range(gsz):
                i = i0 + jj
                if i + 2 < NPAIR:
                    slots[i + 2] = issue_up(i + 2)
                s = slots[i]
                res = []
                for ab in range(2):
                    j = i + ab * NPAIR
                    ps, pk = next_pf()

                    def f(e):
                        for k in range(8):
                            ins = e.matmul(ps[:, 0:T], lhsT=wups[s][:, k, ab * 128:(ab + 1) * 128], rhs=hT[:, k, :],
                                           start=(k == 0), stop=(k == 7))
                        return ins
                    S.op("pe", f, reads=hT_all + [("wups", s)], writes=[pk])
                    ui = (2 * i + ab) % 4
                    up = upb_[ui]
                    upk = ("up", ui)
                    tc_ = tcv[ui]
                    tk = ("tcv", ui)
                    S.op("pool", lambda e: e.tensor_copy(up[:, 0:2], L.chalo[:, j, :]), reads=[("chalo", L.l, j)], writes=[upk])
                    S.op("act", lambda e: e.activation(out=up[:, 2:2 + T], in_=ps[:, 0:T], func=AF.Copy), reads=[pk], writes=[upk])
                    S.op("pool", lambda e: e.tensor_copy(L.chalo[:, j, :], up[:, T:T + 2]), reads=[upk], writes=[("chalo", L.l, j)])
                    S.op("act", lambda e: e.activation(out=tc_[:, :], in_=ps[:, 0:T], func=AF.Identity,
                                                       scale=L.cwT[:, 3 * j + 2:3 * j + 3], bias=L.cbT[:, j:j + 1]),
                         reads=[pk, ("cwT", L.l), ("cbT", L.l)], writes=[tk])
                    S.op("dve", lambda e: e.scalar_tensor_tensor(out=tc_[:, :], in0=up[:, 1:1 + T], scalar=L.cwT[:, 3 * j + 1:3 * j + 2],
                                                                 in1=tc_[:, :], op0=ALU.mult, op1=ALU.add),
                         reads=[upk, tk, ("cwT", L.l)], writes=[tk])
                    S.op("dve", lambda e: e.scalar_tensor_tensor(out=tc_[:, :], in0=up[:, 0:T], scalar=L.cwT[:, 3 * j:3 * j + 1],
                                                                 in1=tc_[:, :], op0=ALU.mult, op1=ALU.add),
                         reads=[upk, tk, ("cwT", L.l)], writes=[tk])
                    res.append((tc_, tk))
                sa = sact[i % 2]
                sk_ = ("sact", i % 2)
                S.op("act", lambda e: e.activation(out=sa[:, :], in_=res[0][0][:, :], func=AF.Silu), reads=[res[0][1]], writes=[sk_])
                S.op("pool", lambda e: e.tensor_tensor(out=hidg[hs][:, jj, :], in0=sa[:, :], in1=res[1][0][:, :], op=ALU.mult),
                     reads=[sk_, res[1][1]], writes=[("hidg", hs)])
            for blk in range(NB):
                for half in range(2):
                    ps, pk = next_pf()

                    def f(e):
                        for jj in range(gsz):
                            ins = e.matmul(ps[:, :], lhsT=hidg[hs][:, jj, blk * 128:(blk + 1) * 128],
                                           rhs=wdng[ds][:, jj, half * 512:(half + 1) * 512], start=(jj == 0), stop=(jj == gsz - 1))
                        return ins
                    S.op("pe", f, reads=[("hidg", hs), ("wdng", ds)], writes=[pk])
                    S.op("dve", lambda e: e.tensor_tensor(out=xt[:, blk, half * 512:(half + 1) * 512],
                                                          in0=xt[:, blk, half * 512:(half + 1) * 512], in1=ps[:, :], op=ALU.add),
                         reads=[pk, xk], writes=[xk])
            i0 += gsz

    def finish(t, xt, xk, st_tile=-2):
        if st_tile == -2:
            st_tile = t
        norm_stats(xt, xk)
        S.op("dve", lambda e: e.tensor_scalar(out=sfin[:, 0:NB], in0=rstd[:, 0:NB], scalar1=flags[:, 0:1], scalar2=flags[:, 1:2],
                                              op0=ALU.mult, op1=ALU.add), reads=["rstd", "flags"], writes=["sfin"])
        for blk in range(NB):
            S.op("dve", lambda e: e.scalar_tensor_tensor(out=xt[:, blk, :], in0=xt[:, blk, :], scalar=sfin[:, blk:blk + 1],
                                                         in1=fnrow[:, :], op0=ALU.mult, op1=ALU.mult),
                 reads=[xk, "sfin", "fnrow"], writes=[xk])
        if st_tile >= 0:
            S.dma("sp", "st_o%d" % (t % 2), out=out_d[st_tile * T:(st_tile + 1) * T, :].rearrange("(b p) f -> p b f", p=128),
                  in_=xt[:, :, :], reads=[xk], writes=[("outd", st_tile)])

    if wave:
        def load_mw(l_):
            q_ = LS[l_]
            S.dma("sp", "ld_mw0", out=winb[:, :, :].rearrange("p k n -> p (k n)"), in_=q_.win_s, reads=[("mws", l_, 0)], writes=["winb"])
            S.dma("sp", "ld_mw1", out=woutb[:, :, :].rearrange("p k n -> p (k n)"), in_=q_.wout_s, reads=[("mws", l_, 1)], writes=["woutb"])
            S.dma("sp", "ld_mw2", out=wpoolb[:, :, :].rearrange("p g d -> p (g d)"), in_=q_.wpool_s, reads=[("mws", l_, 2)],
                  writes=["wpoolb"])

        load_x(0)
        load_mw(0)
        for t in range(NT):
            xt, xk = xts[t % 2], ("xt", t % 2)
            if t + 1 < NT:
                load_x(t + 1)
            for l_ in range(2):
                setL(l_)
                mixer(t, xt, xk)
                load_mw(1 - l_)
                ffn(t, xt, xk)
            finish(t, xt, xk)
        S.barrier()
        return nc

    setL(0)
    if fused:
        send = [nc.dram_tensor("send%d" % i, [T, D], F32).ap() for i in range(2)]
        gath = [nc.dram_tensor("gath%d" % i, [2 * T, D], F32).ap() for i in range(2)]
        xa = sb("xa", [128, NB, D])
        xg = sb("xg", [128, NB, D])
        coefB = sb("coefB", [128, 64])
        faw = sb("faw", [128, 4])
        for g, w in enumerate(WINDOWS):
            S.op("dve", lambda e: e.tensor_scalar(out=faw[:, g:g + 1], in0=flags[:, 2:3], scalar1=1.0 / w, scalar2=None, op0=ALU.mult),
                 reads=["flags"], writes=[("faw", g)])
            S.op("dve", lambda e: e.tensor_scalar(out=coefB[:, g * 16:(g + 1) * 16], in0=invc[:, g * 16:(g + 1) * 16],
                                                  scalar1=flags[:, 3:4], scalar2=faw[:, g:g + 1], op0=ALU.mult, op1=ALU.add),
                 reads=["invc", "flags", ("faw", g)], writes=[("coefB", g)])
        xt, xk = xts[0], ("xt", 0)
        fl = lambda a: a[:, :, :].rearrange("p b f -> p (b f)")

        def load_xa(i):
            ti = min(i, NT - 1)
            S.dma("sp", "ld_xa", out=xa[:, :, :], in_=x_d[ti * T:(ti + 1) * T, :].rearrange("(b p) f -> p b f", p=128), writes=["xa"])

        load_xa(0)
        for i in range(NT + 1):
            S.op("dve", lambda e: e.tensor_scalar(out=fl(xt), in0=fl(xa), scalar1=flags[:, 2:3], scalar2=None, op0=ALU.mult),
                 reads=["xa", "flags"], writes=[xk])
            if i >= 1:
                gk = ("gath", (i - 1) % 2)
                S.dma("sp", "ld_xg", out=xg[:, :, :], in_=gath[(i - 1) % 2][0:T, :].rearrange("(b p) f -> p b f", p=128),
                      reads=[gk], writes=["xg"])
                S.op("dve", lambda e: e.scalar_tensor_tensor(out=fl(xt), in0=fl(xg), scalar=flags[:, 3:4], in1=fl(xt),
                                                             op0=ALU.mult, op1=ALU.add), reads=["xg", "flags", xk], writes=[xk])
            if i + 1 <= NT:
                load_xa(i + 1)
            mixer(i, xt, xk)
            ffn(i, xt, xk)
            finish(i, xt, xk, st_tile=i - 1)
            if i == 0:
                S.op("dve", lambda e: e.tensor_scalar(out=L.Sst[:, :, :].rearrange("p a b -> p (a b)"),
                                                      in0=L.Sst[:, :, :].rearrange("p a b -> p (a b)"), scalar1=flags[:, 2:3],
                                                      scalar2=None, op0=ALU.mult), reads=[("Sst", L.l), "flags"], writes=[("Sst", L.l)])
                S.op("dve", lambda e: e.tensor_scalar(out=L.ubuf[:, :, 0:16], in0=L.ubuf[:, :, 0:16], scalar1=flags[:, 2:3],
                                                      scalar2=None, op0=ALU.mult), reads=[("ub", L.l, g) for g in range(4)] + ["flags"],
                     writes=[("ub", L.l, g) for g in range(4)])
                S.op("dve", lambda e: e.tensor_scalar(out=L.chalo[:, :, :].rearrange("p a b -> p (a b)"),
                                                      in0=L.chalo[:, :, :].rearrange("p a b -> p (a b)"), scalar1=flags[:, 2:3],
                                                      scalar2=None, op0=ALU.mult), reads=[("chalo", L.l, j) for j in range(44)] + ["flags"],
                     writes=[("chalo", L.l, j) for j in range(44)])
            if i < NT:
                sk_ = ("send", i % 2)
                S.dma("sp", "st_send%d" % (i % 2), out=send[i % 2].rearrange("(b p) f -> p b f", p=128), in_=xt[:, :, :],
                      reads=[xk], writes=[sk_])
                S.custom("pool", "ag%d" % (i % 2),
                         lambda e: e.collective_compute("AllGather", ALU.bypass, replica_groups=groups, ins=[send[i % 2]],
                                                        outs=[gath[i % 2]]),
                         reads=[sk_], writes=[("gath", i % 2)])
        S.barrier()
        return nc

    load_x(0)
    for t in range(NT):
        xt, xk = xts[t % 2], ("xt", t % 2)
        if t + 1 < NT:
            load_x(t + 1)
        if stop == "norm":
            norm_to_hT(xt, xk, 0, 8)
            S.barrier()
            dump("hT", hT[:, :, :], [128, 8, T], BF16)
            dump("rstd", rstd[:, :], [128, 4])
            S.barrier()
            return nc
        try:
            mixer(t, xt, xk)
        except _Stop:
            S.barrier()
            dump("xt", xt[:, :, :], [128, NB, D])
            S.barrier()
            return nc
        if stop == "mixer":
            S.barrier()
            dump("xt", xt[:, :, :], [128, NB, D])
            dump("ymix", ymix[:, :, :], [128, 8, T], BF16)
            dump("oT", oT[:, :, :], [128, 4, T])
            dump("qpp", qpp[:, :, :], [128, 4, T], BF16)
            dump("kpp", kpp[:, :, :], [128, 2, T], BF16)
            dump("gtm", gtm[:, :, :], [128, NB, 256])
            dump("srT", srT[:, :, :], [128, 4, T])
            dump(("Sst", L.l), L.Sst[:, :, :], [128, 2, 256])
            dump("vb", vb[:, :, :], [128, NB, 512], BF16)
            S.barrier()
            return nc
        ffn(t, xt, xk)
        finish(t, xt, xk)
    S.barrier()
    return nc


def _consts():
    tri = np.triu(np.ones((128, 128), np.float32))
    invc = np.zeros((128, 64), np.float32)
    for g, w in enumerate(WINDOWS):
        for t in range(16):
            invc[:, g * 16 + t] = 1.0 / min(t + 1, w)
    hmk = np.zeros((128, 2), np.float32)
    hmk[:64, 0] = 1.0
    hmk[64:, 1] = 1.0
    return {"hmask": hmk, "tri": tri, "maskrep": np.ascontiguousarray(np.tile(tri, (1, 4))), "ident": np.eye(128, dtype=np.float32), "invcnt": invc}


def _core_inputs(xin, b, l, final, P):
    f32 = np.float32
    tp = lambda v, n: np.ascontiguousarray(np.asarray(v, f32).reshape(n, 128).T)
    wg2a = np.zeros((32, 256), f32)
    wg2a[0:16] = P["w_gate2"][l]
    wg2a[16] = P["b_gate"][l]
    ab = np.asarray(P["ada_b"][l], f32)
    m = {
        "x": np.ascontiguousarray(xin, dtype=f32),
        "cT": tp(P["c"][b], 8),
        "ada_w": np.ascontiguousarray(P["ada_w"][l], dtype=f32),
        "ada_bT": tp(ab, 48),
        "ada_bg": np.ascontiguousarray(np.concatenate([ab[2048:3072], ab[5120:6144]])[None, :]),
        "w_in": np.ascontiguousarray(P["w_in"][l], dtype=f32),
        "wg2a": wg2a,
        "w_pool": np.ascontiguousarray(np.transpose(np.asarray(P["w_pool"][l], f32), (1, 0, 2))),
        "pool_scT": tp(P["pool_scale"][l], 4),
        "gla_nT": tp(P["gla_norm"][l], 4),
        "w_out": np.ascontiguousarray(P["w_out"][l], dtype=f32),
        "w_up": np.ascontiguousarray(P["w_up"][l], dtype=f32),
        "conv_wT": np.ascontiguousarray(np.transpose(np.asarray(P["conv_w"][l], f32).reshape(3, 44, 128), (2, 1, 0)).reshape(128, 132)),
        "conv_bT": tp(P["conv_b"][l], 44),
        "w_down": np.ascontiguousarray(P["w_down"][l], dtype=f32),
        "fnrow": np.ascontiguousarray(np.broadcast_to(np.asarray(P["final_norm"], f32)[None, :], (128, D))) if final
        else np.ones((128, D), f32),
        "flags": np.ascontiguousarray(np.broadcast_to(np.array(([1.0, 0.0] if final else [0.0, 1.0]) +
                                                                ([1.0, 0.0] if l == 0 else [0.0, 1.0]), f32)[None, :], (128, 4))),
    }
    m.update(_consts())
    return m


_PER_LAYER = ("ada_w", "ada_bT", "ada_bg", "w_in", "wg2a", "w_pool", "pool_scT", "gla_nT", "w_out", "w_up", "conv_wT", "conv_bT",
              "w_down")


def _core_inputs_wave(xin, b, P):
    m0 = _core_inputs(xin, b, 0, True, P)
    m1 = _core_inputs(xin, b, 1, True, P)
    m = {k: v for k, v in m1.items() if k not in _PER_LAYER}
    for k in _PER_LAYER:
        m[k + "0"] = m0[k]
        m[k + "1"] = m1[k]
    return m


_NC_CACHE = {}


def _get_nc(NT):
    if NT not in _NC_CACHE:
        _NC_CACHE[NT] = build_program(NT)
    return _NC_CACHE[NT]


def run_layer(xs, l, final, P, NT):
    nc = build_program(NT)
    in_maps = [_core_inputs(xs[b], b, l, final, P) for b in range(len(xs))]
    in_maps = in_maps + in_maps
    res = run_bass_kernel_spmd(nc, in_maps, core_ids=list(range(8)))
    return [np.asarray(r["out"]) for r in res.results[:len(xs)]]


def kernel(**inputs):
    P = {k: np.asarray(v) for k, v in inputs.items()}
    x = P["x"]
    NT = SEQ // T
    nc = build_program(NT, wave=True)
    in_maps = [_core_inputs_wave(x[b], b, P) for b in range(BATCH)]
    in_maps = in_maps + in_maps
    res = run_bass_kernel_spmd(nc, in_maps, core_ids=list(range(8)))
    return np.stack([np.asarray(res.results[b]["out"]) for b in range(BATCH)], axis=0).astype(np.float32)
```

```python
import numpy as np
import ml_dtypes
from contextlib import ExitStack
import concourse.bass as bass
import concourse.mybir as mybir
from concourse.bass_utils import run_bass_kernel_spmd

F32 = mybir.dt.float32
BF16 = mybir.dt.bfloat16
AF = mybir.ActivationFunctionType
ALU = mybir.AluOpType

D = 1024
SEQ = 8192
BATCH = 4
DEPTH = 2
T = 256
NB = T // 128
DIN = 2064
DFF = 2816
NPAIR = 22
EPS = 1e-6
GROUPS = (6, 6, 5, 5)
WINDOWS = (2, 4, 8, 16)


class Sched:
    def __init__(self, nc):
        self.nc = nc
        self.e = {"pe": nc.tensor, "act": nc.scalar, "dve": nc.vector, "pool": nc.gpsimd, "sp": nc.sync}
        self.semh = {k: nc.alloc_semaphore(name="s_" + k) for k in self.e}
        self.cnt = {k: 0 for k in self.e}
        self.seen = {k: {} for k in self.e}
        self.lastw = {}
        self.readers = {}

    def _sem(self, sk):
        if sk not in self.semh:
            self.semh[sk] = self.nc.alloc_semaphore(name="d_" + str(len(self.semh)))
            self.cnt[sk] = 0
        return self.semh[sk]

    def _need(self, eng, deps):
        best = {}
        for sk, v in deps:
            if v > best.get(sk, 0):
                best[sk] = v
        for sk, v in best.items():
            if self.seen[eng].get(sk, 0) < v:
                self.e[eng].wait_ge(self.semh[sk], v)
                self.seen[eng][sk] = v

    def _deps(self, reads, writes):
        deps = []
        for k in reads:
            if k in self.lastw:
                deps.append(self.lastw[k])
        for k in writes:
            if k in self.lastw:
                deps.append(self.lastw[k])
            deps.extend(self.readers.get(k, {}).items())
        return deps

    def _commit(self, reads, writes, tag):
        for k in reads:
            r = self.readers.setdefault(k, {})
            if tag[1] > r.get(tag[0], 0):
                r[tag[0]] = tag[1]
        for k in writes:
            self.lastw[k] = tag
            self.readers[k] = {}

    def op(self, eng, fn, reads=(), writes=()):
        self._need(eng, self._deps(reads, writes))
        ins = fn(self.e[eng])
        self.cnt[eng] += 1
        ins.then_inc(self.semh[eng], 1)
        self._commit(reads, writes, (eng, self.cnt[eng]))

    def dma(self, eng, sk, out, in_, reads=(), writes=(), nodep_writes=()):
        self._sem(sk)
        self._need(eng, self._deps(reads, writes))
        ins = self.e[eng].dma_start(out=out, in_=in_)
        self.cnt[sk] += 16
        ins.then_inc(self.semh[sk], 16)
        self._commit(reads, list(writes) + list(nodep_writes), (sk, self.cnt[sk]))

    def custom(self, eng, sk, fn, reads=(), writes=()):
        self._sem(sk)
        self._need(eng, self._deps(reads, writes))
        ins = fn(self.e[eng])
        self.cnt[sk] += 16
        ins.then_inc(self.semh[sk], 16)
        self._commit(reads, writes, (sk, self.cnt[sk]))

    def barrier(self, engines=None):
        for eng in engines or self.e:
            self._need(eng, [(sk, c) for sk, c in self.cnt.items() if c > 0])


def build_program(NT, stop=None, fused=False, groups=None, wave=False):
    S_tok = NT * T
    nc = bass.Bass("TRN2", target_bir_lowering=False)

    def din(name, shape, dt=F32):
        return nc.dram_tensor(name, list(shape), dt, kind="ExternalInput").ap()

    from types import SimpleNamespace
    NL = 2 if wave else 1
    x_d = din("x", [S_tok, D])
    cT_d = din("cT", [128, 8])
    LS = []
    for l_ in range(NL):
        sf = str(l_) if wave else ""
        LS.append(SimpleNamespace(
            l=l_,
            adaw_d=din("ada_w" + sf, [D, 6 * D]), adabT_d=din("ada_bT" + sf, [128, 48]), adabg_d=din("ada_bg" + sf, [1, 2048]),
            win_d=din("w_in" + sf, [D, DIN]), wg2a_d=din("wg2a" + sf, [32, 256]), wpool_d=din("w_pool" + sf, [128, 4, 128]),
            psc_d=din("pool_scT" + sf, [128, 4]), gln_d=din("gla_nT" + sf, [128, 4]), wout_d=din("w_out" + sf, [D, D]),
            wup_d=din("w_up" + sf, [D, 2 * DFF]), cw_d=din("conv_wT" + sf, [128, 132]), cb_d=din("conv_bT" + sf, [128, 44]),
            wdn_d=din("w_down" + sf, [DFF, D])))
    L = SimpleNamespace()

    def setL(l_):
        L.__dict__.update(LS[l_].__dict__)
    fn_d = din("fnrow", [128, D])
    flag_d = din("flags", [128, 4])
    tri_d = din("tri", [128, 128])
    mask_d = din("maskrep", [128, 512])
    ident_d = din("ident", [128, 128])
    invc_d = din("invcnt", [128, 64])
    hm_d = din("hmask", [128, 2])
    out_d = nc.dram_tensor("out", [S_tok, D], F32, kind="ExternalOutput").ap()
    for l_ in range(NL):
        LS[l_].wup_s = nc.dram_tensor("wup_s%d" % l_, [NPAIR, 128, 8, 256], BF16).ap()
        LS[l_].wdn_s = nc.dram_tensor("wdn_s%d" % l_, [NPAIR, 128, 1024], BF16).ap()
        LS[l_].win_s = nc.dram_tensor("win_s%d" % l_, [128, 8 * DIN], BF16).ap()
        LS[l_].wout_s = nc.dram_tensor("wout_s%d" % l_, [128, 8 * D], BF16).ap()
        LS[l_].wpool_s = nc.dram_tensor("wpool_s%d" % l_, [128, 512], BF16).ap()

    S = Sched(nc)
    dbg = {}

    def dump(name, ap, shape, dt=F32, key=None):
        d = nc.dram_tensor("dbg_" + name, list(shape), dt, kind="ExternalOutput").ap()
        S.dma("sp", "dbg_" + name, out=d, in_=ap, reads=[key] if key is not None else [])

    class _Stop(Exception):
        pass
    sb = lambda name, shape, dt=F32: nc.alloc_sbuf_tensor("sb_" + name, list(shape), dt)

    winb = sb("winb", [128, 8, DIN], BF16)
    woutb = sb("woutb", [128, 8, D], BF16)
    wpoolb = sb("wpoolb", [128, 4, 128], BF16)
    for l_ in range(NL):
        q_ = LS[l_]
        n_ = lambda nm: nm + str(l_)
        q_.modT = sb(n_("modT"), [128, 48])
        q_.pscT = sb(n_("pscT"), [128, 4])
        q_.glnT = sb(n_("glnT"), [128, 4])
        q_.cwT = sb(n_("cwT"), [128, 132])
        q_.cbT = sb(n_("cbT"), [128, 44])
        q_.wg2a = sb(n_("wg2a"), [32, 256])
        q_.chalo = sb(n_("chalo"), [128, 44, 2])
        q_.ubuf = sb(n_("ubuf"), [128, 4, 16 + T])
        q_.Sst = sb(n_("Sst"), [128, 2, 256])
    fnrow = sb("fnrow", [128, D])
    flags = sb("flags", [128, 4])
    tri = sb("tri", [128, 128])
    maskrep = sb("maskrep", [128, 512])
    identf = sb("identf", [128, 128])
    identb = sb("identb", [128, 128], BF16)
    invc = sb("invc", [128, 64])
    hm = sb("hm", [128, 2])
    onesD = sb("onesD", [128, 128])
    epsT = sb("epsT", [128, 1])
    zaug = sb("zaug", [32, T])

    pf = [nc.alloc_psum_tensor("pf%d" % i, [128, 512], F32) for i in range(6)]
    pt = [nc.alloc_psum_tensor("pt%d" % i, [128, 1024], BF16) for i in range(2)]
    rr = {"pf": 0, "pt": 0}

    def next_pf():
        i = rr["pf"]
        rr["pf"] = (i + 1) % len(pf)
        return pf[i], ("pf", i)

    def next_pt():
        i = rr["pt"]
        rr["pt"] = (i + 1) % len(pt)
        return pt[i], ("pt", i)

    def load_const(dst, src, key):
        S.dma("sp", "ld_" + key, out=dst, in_=src, writes=[key])

    load_const(fnrow[:, :], fn_d, "fnrow")
    load_const(flags[:, :], flag_d, "flags")
    load_const(tri[:, :], tri_d, "tri")
    load_const(maskrep[:, :], mask_d, "maskrep")
    load_const(identf[:, :], ident_d, "identf")
    load_const(invc[:, :], invc_d, "invc")
    load_const(hm[:, :], hm_d, "hm")
    S.op("dve", lambda e: e.tensor_copy(identb[:, :], identf[:, :]), reads=["identf"], writes=["identb"])
    S.op("dve", lambda e: e.memset(onesD[:, :], 1.0 / 128.0), writes=["onesD"])
    S.op("dve", lambda e: e.memset(epsT[:, :], EPS), writes=["epsT"])
    for l_ in range(NL):
        setL(l_)
        S.dma("sp", "ld_pscT%d" % l_, out=L.pscT[:, :], in_=L.psc_d, writes=[("pscT", L.l)])
        S.dma("sp", "ld_glnT%d" % l_, out=L.glnT[:, :], in_=L.gln_d, writes=[("glnT", L.l)])
        S.dma("sp", "ld_cwT%d" % l_, out=L.cwT[:, :], in_=L.cw_d, writes=[("cwT", L.l)])
        S.dma("sp", "ld_cbT%d" % l_, out=L.cbT[:, :], in_=L.cb_d, writes=[("cbT", L.l)])
        S.dma("sp", "ld_wg2a%d" % l_, out=L.wg2a[:, :], in_=L.wg2a_d, writes=[("wg2a", L.l)])
        S.op("dve", lambda e: e.memset(L.chalo[:, :, :], 0.0), writes=[("chalo", L.l, j) for j in range(44)])
        S.op("dve", lambda e: e.memset(L.ubuf[:, :, :], 0.0), writes=[("ub", L.l, g) for g in range(4)])
        S.op("dve", lambda e: e.memset(L.Sst[:, :, :], 0.0), writes=[("Sst", L.l)])
    S.op("dve", lambda e: e.memset(zaug[:, :], 1.0), writes=["zaug"])

    with ExitStack() as ps_:
        pb = lambda name, shape, dt=F32: ps_.enter_context(nc.sbuf_tensor("pb_" + name, list(shape), dt))
        cT = pb("cT", [128, 8])
        cact = pb("cact", [128, 8])
        ones128 = pb("ones128", [128, 128])
        crep = pb("crep", [128, 8, 128])
        adabT = pb("adabT", [128, 48])
        adabg = pb("adabg", [1, 2048])
        Gt = [pb("G1t", [128, D]), pb("G2t", [128, D])]
        ablk = [pb("ablk%d" % i, [128, 8, 512]) for i in range(2)]
        stg = [pb("stg%d" % i, [128, 2 * DFF]) for i in range(2)]
        cvt = [pb("cvt%d" % i, [128, 2 * DFF], BF16) for i in range(2)]

        load_const(cT[:, :], cT_d, "cT")
        S.op("act", lambda e: e.activation(out=cact[:, :], in_=cT[:, :], func=AF.Silu), reads=["cT"], writes=["cact"])
        S.op("dve", lambda e: e.memset(ones128[:, :], 1.0), writes=["ones128"])
        for k in range(8):
            S.op("dve", lambda e: e.tensor_scalar(out=crep[:, k, :], in0=ones128[:, :], scalar1=cact[:, k:k + 1],
                                                  scalar2=None, op0=ALU.mult),
                 reads=["ones128", "cact"], writes=[("crep", k)])
        for l_ in range(NL):
            setL(l_)
            S.dma("sp", "ld_adabT", out=adabT[:, :], in_=L.adabT_d, writes=["adabT"])
            S.dma("sp", "ld_adabg", out=adabg[:, :], in_=L.adabg_d, writes=["adabg"])
            for nb in range(12):
                a = ablk[nb % 2]
                ak = ("ablk", nb % 2)
                S.dma("sp", "ld_ablk%d" % (nb % 2), out=a[:, :, :],
                      in_=L.adaw_d[:, nb * 512:(nb + 1) * 512].rearrange("(k p) n -> p k n", p=128), writes=[ak])
                ps, pk = next_pf()
                if nb in (4, 5, 10, 11):
                    gi = 0 if nb < 6 else 1
                    half = nb % 2
                    boff = (0 if nb < 6 else 1024) + half * 512

                    def f(e):
                        for k in range(8):
                            e.matmul(ps[:, :], lhsT=crep[:, k, :], rhs=a[:, k, :], start=(k == 0), stop=False)
                        return e.matmul(ps[:, :], lhsT=ones128[0:1, :], rhs=adabg[0:1, boff:boff + 512], start=False, stop=True)
                    S.op("pe", f, reads=[ak, "ones128", "adabg"] + [("crep", k) for k in range(8)], writes=[pk])
                    S.op("act", lambda e: e.activation(out=Gt[gi][:, half * 512:(half + 1) * 512], in_=ps[:, :], func=AF.Copy),
                         reads=[pk], writes=[("Gt", gi, half)])
                else:
                    def f(e):
                        for m in range(4):
                            for k in range(8):
                                ins = e.matmul(ps[:, m:m + 1], lhsT=a[:, k, m * 128:(m + 1) * 128], rhs=cact[:, k:k + 1],
                                               start=(k == 0), stop=(k == 7))
                        return ins
                    S.op("pe", f, reads=[ak, "cact"], writes=[pk])
                    S.op("dve", lambda e: e.tensor_tensor(out=L.modT[:, 4 * nb:4 * nb + 4], in0=ps[:, 0:4],
                                                          in1=adabT[:, 4 * nb:4 * nb + 4], op=ALU.add),
                         reads=[pk, "adabT"], writes=[("modT", L.l, nb)])
            for nb in (2, 3, 8, 9):
                S.op("dve", lambda e: e.tensor_scalar(out=L.modT[:, 4 * nb:4 * nb + 4], in0=L.modT[:, 4 * nb:4 * nb + 4],
                                                      scalar1=1.0, scalar2=None, op0=ALU.add),
                     reads=[("modT", L.l, nb)], writes=[("modT", L.l, nb)])
            ci = [0]

            def stage_load(src, ncols):
                i = ci[0] % 2
                ci[0] += 1
                S.dma("sp", "ld_stg%d" % i, out=stg[i][:, 0:ncols], in_=src, writes=[("stg", i)])
                return i

            for k in range(8):
                i = stage_load(L.win_d[k * 128:(k + 1) * 128, :], DIN)
                eng = "act" if k % 2 == 0 else "dve"
                if eng == "act":
                    S.op("act", lambda e: e.activation(out=winb[:, k, :], in_=stg[i][:, 0:DIN], func=AF.Copy),
                         reads=[("stg", i)], writes=["winb"])
                else:
                    S.op("dve", lambda e: e.tensor_copy(winb[:, k, :], stg[i][:, 0:DIN]), reads=[("stg", i)], writes=["winb"])
            for k in range(8):
                i = stage_load(L.wout_d[k * 128:(k + 1) * 128, :], D)
                S.op("dve", lambda e: e.tensor_tensor(out=woutb[:, k, :], in0=stg[i][:, 0:D], in1=Gt[0][:, :], op=ALU.mult),
                     reads=[("stg", i), ("Gt", 0, 0), ("Gt", 0, 1)], writes=["woutb"])
            i = stage_load(L.wpool_d.rearrange("p g d -> p (g d)"), 512)
            S.op("dve", lambda e: e.tensor_copy(wpoolb[:, :, :].rearrange("p g d -> p (g d)"), stg[i][:, 0:512]),
                 reads=[("stg", i)], writes=["wpoolb"])
            for k in range(8):
                i = stage_load(L.wup_d[k * 128:(k + 1) * 128, :], 2 * DFF)
                c3 = cvt[i][:, :].rearrange("p (i c) -> p i c", c=256)
                S.op("act", lambda e: e.activation(out=c3[:, :, 0:128], in_=stg[i][:, 0:DFF].rearrange("p (i c) -> p i c", c=128),
                                                   func=AF.Copy), reads=[("stg", i)], writes=[("cvt", i, 0)])
                S.op("dve", lambda e: e.tensor_copy(c3[:, :, 128:256], stg[i][:, DFF:2 * DFF].rearrange("p (i c) -> p i c", c=128)),
                     reads=[("stg", i)], writes=[("cvt", i, 1)])
                for hh in range(2):
                    S.dma("sp", "st_scr%d" % i, out=L.wup_s[hh * 11:(hh + 1) * 11, :, k, :].rearrange("i p c -> p i c"),
                          in_=c3[:, hh * 11:(hh + 1) * 11, :], reads=[("cvt", i, 0), ("cvt", i, 1)], nodep_writes=["scratch%d" % i])
            for j in range(NPAIR):
                i = stage_load(L.wdn_d[j * 128:(j + 1) * 128, :], D)
                S.op("dve", lambda e: e.tensor_tensor(out=cvt[i][:, 0:D], in0=stg[i][:, 0:D], in1=Gt[1][:, :], op=ALU.mult),
                     reads=[("stg", i), ("Gt", 1, 0), ("Gt", 1, 1)], writes=[("cvt", i, 0), ("cvt", i, 1)])
                S.dma("sp", "st_scr%d" % i, out=L.wdn_s[j], in_=cvt[i][:, 0:D],
                      reads=[("cvt", i, 0), ("cvt", i, 1)], nodep_writes=["scratch%d" % i])
            if wave:
                S.dma("sp", "st_mw0", out=L.win_s, in_=winb[:, :, :].rearrange("p k n -> p (k n)"), reads=["winb"],
                      nodep_writes=[("mws", L.l, 0)])
                S.dma("sp", "st_mw1", out=L.wout_s, in_=woutb[:, :, :].rearrange("p k n -> p (k n)"), reads=["woutb"],
                      nodep_writes=[("mws", L.l, 1)])
                S.dma("sp", "st_mw2", out=L.wpool_s, in_=wpoolb[:, :, :].rearrange("p g d -> p (g d)"), reads=["wpoolb"],
                      nodep_writes=[("mws", L.l, 2)])
        S.barrier()
        if stop == "prep":
            S.barrier()
            dump("winb", winb[:, :, :], [128, 8, DIN], BF16)
            dump("woutb", woutb[:, :, :], [128, 8, D], BF16)
            dump("modT", L.modT[:, :], [128, 48])
            dump("G1t", Gt[0][:, :], [128, D])
            S.barrier()
            return nc

    xts = [sb("xt%d" % i, [128, NB, D]) for i in range(1 if fused else 2)]
    xnb = sb("xnb", [128, NB, D], BF16)
    junk = sb("junk", [128, D], BF16)
    hT = sb("hT", [128, 8, T], BF16)
    ss = sb("ss", [128, 4])
    rstd = sb("rstd", [128, 4])
    sfin = sb("sfin", [128, 4])
    wups = [sb("wups%d" % i, [128, 8, 256], BF16) for i in range(3)]
    wdng = [sb("wdng%d" % i, [128, 6, D], BF16) for i in range(2)]
    hidg = [sb("hidg%d" % i, [128, 6, T], BF16) for i in range(2)]
    pA = sb("pA", [128, 16 + T])
    pB = sb("pB", [128, 16 + T])
    tmp16 = sb("tmp16", [128, 16])
    dT = sb("dT", [128, 4, T], BF16)
    qT = sb("qT", [128, 2, T])
    kT = sb("kT", [128, 2, T])
    srT = sb("srT", [128, 4, T])
    vb = sb("vb", [128, NB, 512], BF16)
    gtm = sb("gtm", [128, NB, 256])
    e1 = sb("e1", [128, 2, T])
    e2 = sb("e2", [128, 2, T])
    pbias = sb("pbias", [128, 2 * NB])
    nbias = sb("nbias", [128, 2 * NB])
    em = sb("em", [128, 2 * NB])
    dl = sb("dl", [128, 2 * NB])
    qpp = sb("qpp", [128, 4, T], BF16)
    kpp = sb("kpp", [128, 2, T], BF16)
    kptok = sb("kptok", [128, NB, 256], BF16)
    am = sb("am", [128, 512], BF16)
    Smb = sb("Smb", [128, 2, 256], BF16)
    tmpU = sb("tmpU", [128, 2, 256])
    oT = sb("oT", [128, 4, T])
    osq = sb("osq", [128, 4, T])
    rso = sb("rso", [128, 4, T])
    otmp = osq
    ymix = sb("ymix", [128, 8, T], BF16)
    upb_ = [sb("up%d" % i, [128, 2 + T]) for i in range(4)]
    tcv = [sb("tcv%d" % i, [128, T]) for i in range(4)]
    sact = [sb("sact%d" % i, [128, T]) for i in range(2)]

    def load_x(t):
        S.dma("sp", "ld_x%d" % (t % 2), out=xts[t % 2][:, :, :],
              in_=x_d[t * T:(t + 1) * T, :].rearrange("(b p) f -> p b f", p=128), writes=[("xt", t % 2)])

    def norm_stats(xt, xk):
        S.op("dve", lambda e: e.memset(ss[:, :], 0.0), writes=["ss"])
        for blk in range(NB):
            S.op("act", lambda e: e.activation(out=junk[:, :], in_=xt[:, blk, :], func=AF.Square,
                                               accum_out=ss[:, blk:blk + 1]), reads=[xk], writes=["junk", "ss"])
        S.op("act", lambda e: e.activation(out=rstd[:, 0:NB], in_=ss[:, 0:NB], func=AF.Sqrt, scale=1.0 / D, bias=epsT[:, 0:1]),
             reads=["ss", "epsT"], writes=["rstd"])
        S.op("dve", lambda e: e.reciprocal(rstd[:, 0:NB], rstd[:, 0:NB]), reads=["rstd"], writes=["rstd"])

    def norm_to_hT(xt, xk, sh_off, sc_off):
        norm_stats(xt, xk)
        for blk in range(NB):
            S.op("dve", lambda e: e.tensor_scalar(out=xnb[:, blk, :], in0=xt[:, blk, :], scalar1=rstd[:, blk:blk + 1],
                                                  scalar2=None, op0=ALU.mult),
                 reads=[xk, "rstd"], writes=[("xnb", blk)])
        for k in range(8):
            p, pk = next_pt()

            def f(e):
                for blk in range(NB):
                    ins = e.transpose(out=p[:, blk * 128:(blk + 1) * 128], in_=xnb[:, blk, k * 128:(k + 1) * 128],
                                      identity=identb[:, :])
                return ins
            S.op("pe", f, reads=[("xnb", b) for b in range(NB)] + ["identb"], writes=[pk])
            S.op("act", lambda e: e.activation(out=hT[:, k, :], in_=p[:, 0:T], func=AF.Identity,
                                               scale=L.modT[:, sc_off + k:sc_off + k + 1],
                                               bias=L.modT[:, sh_off + k:sh_off + k + 1]),
                 reads=[pk] + [("modT", L.l, i) for i in range(12)], writes=[("hT", k)])

    hT_all = [("hT", k) for k in range(8)]

    def proj_fm(col, M):
        ps, pk = next_pf()

        def f(e):
            for k in range(8):
                ins = e.matmul(ps[0:M, 0:T], lhsT=winb[:, k, col:col + M], rhs=hT[:, k, :], start=(k == 0), stop=(k == 7))
            return ins
        S.op("pe", f, reads=hT_all + ["winb"], writes=[pk])
        return ps, pk

    def mixer(t, xt, xk):
        norm_to_hT(xt, xk, 0, 8)
        for g in range(4):
            ps, pk = proj_fm(g * 128, 128)
            S.op("dve", lambda e: e.tensor_copy(L.ubuf[:, g, 16:16 + T], ps[:, 0:T]), reads=[pk], writes=[("ub", L.l, g)])
        for c in range(2):
            ps, pk = proj_fm(512 + c * 128, 128)
            S.op("dve", lambda e: e.tensor_scalar(out=qT[:, c, :], in0=ps[:, 0:T], scalar1=0.125, scalar2=None, op0=ALU.mult),
                 reads=[pk], writes=[("qT", c)])
        for c in range(2):
            ps, pk = proj_fm(768 + c * 128, 128)
            S.op("dve", lambda e: e.tensor_copy(kT[:, c, :], ps[:, 0:T]), reads=[pk], writes=[("kT", c)])
        for c in range(4):
            ps, pk = proj_fm(1536 + c * 128, 128)
            S.op("act", lambda e: e.activation(out=srT[:, c, :], in_=ps[:, 0:T], func=AF.Silu), reads=[pk], writes=[("srT", c)])
        ps, pk = proj_fm(2048, 16)
        S.op("dve", lambda e: e.tensor_copy(zaug[0:16, :], ps[0:16, 0:T]), reads=[pk], writes=["zaug"])
        for blk in range(NB):
            ps, pk = next_pf()

            def f(e):
                for k in range(8):
                    ins = e.matmul(ps[:, :], lhsT=hT[:, k, blk * 128:(blk + 1) * 128], rhs=winb[:, k, 1024:1536],
                                   start=(k == 0), stop=(k == 7))
                return ins
            S.op("pe", f, reads=hT_all + ["winb"], writes=[pk])
            S.op("act", lambda e: e.activation(out=vb[:, blk, :], in_=ps[:, :], func=AF.Copy), reads=[pk], writes=[("vb", blk)])

        if stop == "inproj":
            raise _Stop()
        W = 16 + T
        for g, w in enumerate(WINDOWS):
            ub = L.ubuf[:, g, :]
            uk = ("ub", L.l, g)
            add = lambda o, a, b_: (lambda e: e.tensor_tensor(out=o, in0=a, in1=b_, op=ALU.add))
            S.op("pool", add(pA[:, 1:W], ub[:, 1:W], ub[:, 0:W - 1]), reads=[uk], writes=["pA"])
            cur, ck = pA, "pA"
            if w >= 4:
                S.op("pool", add(pB[:, 3:W], pA[:, 3:W], pA[:, 1:W - 2]), reads=["pA"], writes=["pB"])
                cur, ck = pB, "pB"
            if w >= 8:
                S.op("pool", add(pA[:, 7:W], pB[:, 7:W], pB[:, 3:W - 4]), reads=["pB"], writes=["pA"])
                cur, ck = pA, "pA"
            if w >= 16:
                S.op("pool", add(pB[:, 15:W], pA[:, 15:W], pA[:, 7:W - 8]), reads=["pA"], writes=["pB"])
                cur, ck = pB, "pB"
            S.op("dve", lambda e: e.scalar_tensor_tensor(out=dT[:, g, :], in0=cur[:, 16:W], scalar=1.0 / w, in1=ub[:, 16:W],
                                                         op0=ALU.mult, op1=ALU.subtract), reads=[ck, uk], writes=[("dT", g)])
            if t == 0 or (fused and t == 1):
                cf, cfk = (invc, "invc") if t == 0 else (coefB, ("coefB", g))
                S.op("pool", lambda e: e.tensor_tensor(out=tmp16[:, :], in0=cur[:, 16:32], in1=cf[:, g * 16:(g + 1) * 16],
                                                       op=ALU.mult), reads=[ck, cfk], writes=["tmp16"])
                S.op("pool", lambda e: e.tensor_tensor(out=dT[:, g, 0:16], in0=tmp16[:, :], in1=ub[:, 16:32], op=ALU.subtract),
                     reads=["tmp16", uk], writes=[("dT", g)])
            S.op("pool", lambda e: e.tensor_copy(ub[:, 0:16], ub[:, T:T + 16]), reads=[uk], writes=[uk])
            ps, pk = next_pf()
            S.op("pe", lambda e: e.matmul(ps[:, 0:T], lhsT=wpoolb[:, g, :], rhs=dT[:, g, :], start=True, stop=True),
                 reads=[("dT", g), "wpoolb"], writes=[pk])
            S.op("act", lambda e: e.activation(out=ymix[:, g, :], in_=ps[:, 0:T], func=AF.Identity, scale=L.pscT[:, g:g + 1]),
                 reads=[pk, ("pscT", L.l)], writes=[("ymix", g)])

        if stop == "pool":
            raise _Stop()
        ps, pk = next_pf()

        def f(e):
            for blk in range(NB):
                ins = e.matmul(ps[:, blk * 256:(blk + 1) * 256], lhsT=zaug[0:32, blk * 128:(blk + 1) * 128], rhs=L.wg2a[0:32, :],
                               start=True, stop=True)
            return ins
        S.op("pe", f, reads=["zaug", ("wg2a", L.l)], writes=[pk])
        gflat = gtm[:, :, :].rearrange("p b c -> p (b c)")
        S.op("act", lambda e: e.activation(out=gflat, in_=ps[:, 0:NB * 256], func=AF.Exp, scale=-1.0), reads=[pk], writes=["gtm"])
        S.op("act", lambda e: e.activation(out=gflat, in_=gflat, func=AF.Ln, bias=1.0), reads=["gtm"], writes=["gtm"])
        if stop == "gate":
            raise _Stop()
        cps, ck_ = next_pf()

        def f(e):
            for fc in range(2):
                for blk in range(NB):
                    ins = e.matmul(cps[:, fc * T + blk * 128:fc * T + (blk + 1) * 128],
                                   lhsT=gtm[:, blk, fc * 128:(fc + 1) * 128], rhs=tri[:, :], start=True, stop=True)
            return ins
        S.op("pe", f, reads=["gtm", "tri"], writes=[ck_])
        c4 = cps[:, :].rearrange("p (a c) -> p a c", c=128)
        S.op("dve", lambda e: e.tensor_scalar(out=pbias[:, :], in0=c4[:, :, 63], scalar1=1.0 / 16, scalar2=None, op0=ALU.mult),
             reads=[ck_], writes=["pbias"])
        S.op("dve", lambda e: e.tensor_scalar(out=nbias[:, :], in0=c4[:, :, 63], scalar1=-1.0 / 16, scalar2=None, op0=ALU.mult),
             reads=[ck_], writes=["nbias"])
        S.op("act", lambda e: e.activation(out=em[:, :], in_=c4[:, :, 63], func=AF.Exp, scale=-1.0 / 16), reads=[ck_], writes=["em"])
        S.op("act", lambda e: e.activation(out=dl[:, :], in_=c4[:, :, 127], func=AF.Exp, scale=-1.0 / 16), reads=[ck_], writes=["dl"])
        for fc in range(2):
            for blk in range(NB):
                a = fc * NB + blk
                cs = slice(blk * 128, (blk + 1) * 128)
                S.op("act", lambda e: e.activation(out=e1[:, fc, cs], in_=c4[:, a, :], func=AF.Exp, scale=-1.0 / 16,
                                                   bias=pbias[:, a:a + 1]), reads=[ck_, "pbias"], writes=[("e1", fc)])
                S.op("act", lambda e: e.activation(out=e2[:, fc, cs], in_=c4[:, a, :], func=AF.Exp, scale=1.0 / 16,
                                                   bias=nbias[:, a:a + 1]), reads=[ck_, "nbias"], writes=[("e2", fc)])
            for hl in range(2):
                S.op("dve", lambda e: e.scalar_tensor_tensor(out=qpp[:, 2 * fc + hl, :], in0=qT[:, fc, :], scalar=hm[:, hl:hl + 1],
                                                             in1=e1[:, fc, :], op0=ALU.mult, op1=ALU.mult),
                     reads=[("qT", fc), ("e1", fc), "hm"], writes=[("qpp", 2 * fc + hl)])
            S.op("dve", lambda e: e.tensor_tensor(out=kpp[:, fc, :], in0=kT[:, fc, :], in1=e2[:, fc, :], op=ALU.mult),
                 reads=[("kT", fc), ("e2", fc)], writes=[("kpp", fc)])
        if stop == "cum":
            raise _Stop()
        p, pk = next_pt()

        def f(e):
            for blk in range(NB):
                for fc in range(2):
                    ins = e.transpose(out=p[:, (blk * 2 + fc) * 128:(blk * 2 + fc + 1) * 128],
                                      in_=kpp[:, fc, blk * 128:(blk + 1) * 128], identity=identb[:, :])
            return ins
        S.op("pe", f, reads=[("kpp", 0), ("kpp", 1), "identb"], writes=[pk])
        S.op("dve", lambda e: e.tensor_copy(kptok[:, :, :].rearrange("p b c -> p (b c)"), p[:, 0:NB * 256]),
             reads=[pk], writes=["kptok"])
        if stop == "ktok":
            raise _Stop()
        for blk in range(NB):
            cs = slice(blk * 128, (blk + 1) * 128)
            for fc in range(2):
                a = fc * NB + blk
                S.op("dve", lambda e: e.tensor_scalar(out=Smb[:, fc, :], in0=L.Sst[:, fc, :], scalar1=em[:, a:a + 1], scalar2=None,
                                                      op0=ALU.mult), reads=[("Sst", L.l), "em"], writes=[("Smb", fc)])
            aps, ak = next_pf()

            def f(e):
                for h in range(4):
                    fc, hl = h // 2, h % 2
                    rs = slice(64 * hl, 64 * hl + 64)
                    ins = e.matmul(aps[:, h * 128:(h + 1) * 128], lhsT=kpp[:, fc, cs], rhs=qpp[:, h, cs], start=True, stop=True)
                return ins
            S.op("pe", f, reads=[("kpp", 0), ("kpp", 1)] + [("qpp", h) for h in range(4)], writes=[ak])
            S.op("dve", lambda e: e.tensor_tensor(out=am[:, :], in0=aps[:, :], in1=maskrep[:, :], op=ALU.mult),
                 reads=[ak, "maskrep"], writes=["am"])
            ops_, ok = next_pf()

            def f(e):
                for h in range(4):
                    fc, hl = h // 2, h % 2
                    rs = slice(64 * hl, 64 * hl + 64)
                    e.matmul(ops_[:, h * 128:(h + 1) * 128], lhsT=vb[:, blk, h * 128:(h + 1) * 128], rhs=am[:, h * 128:(h + 1) * 128],
                             start=True, stop=False)
                    ins = e.matmul(ops_[:, h * 128:(h + 1) * 128], lhsT=Smb[:, fc, hl * 128:(hl + 1) * 128], rhs=qpp[:, h, cs],
                                   start=False, stop=True)
                return ins
            S.op("pe", f, reads=[("vb", blk), "am", ("Smb", 0), ("Smb", 1)] + [("qpp", h) for h in range(4)], writes=[ok])
            S.op("act", lambda e: e.activation(out=oT[:, :, cs], in_=ops_[:, :].rearrange("p (h c) -> p h c", c=128), func=AF.Copy),
                 reads=[ok], writes=["oT"])
            ups, uk_ = next_pf()

            def f(e):
                for fc in range(2):
                    ins = e.matmul(ups[:, fc * 256:(fc + 1) * 256], lhsT=kptok[:, blk, fc * 128:(fc + 1) * 128],
                                   rhs=vb[:, blk, fc * 256:(fc + 1) * 256], start=True, stop=True)
                return ins
            S.op("pe", f, reads=["kptok", ("vb", blk)], writes=[uk_])
            for fc in range(2):
                a = fc * NB + blk
                S.op("dve", lambda e: e.tensor_scalar(out=tmpU[:, fc, :], in0=ups[:, fc * 256:(fc + 1) * 256],
                                                      scalar1=e1[:, fc, blk * 128 + 127:blk * 128 + 128], scalar2=None, op0=ALU.mult),
                     reads=[uk_, ("e1", fc)], writes=[("tmpU", fc)])
                S.op("dve", lambda e: e.scalar_tensor_tensor(out=L.Sst[:, fc, :], in0=L.Sst[:, fc, :], scalar=dl[:, a:a + 1],
                                                             in1=tmpU[:, fc, :], op0=ALU.mult, op1=ALU.add),
                     reads=[("Sst", L.l), "dl", ("tmpU", fc)], writes=[("Sst", L.l)])
        if stop == "rec":
            raise _Stop()
        S.op("act", lambda e: e.activation(out=osq[:, :, :].rearrange("p h t -> p (h t)"),
                                           in_=oT[:, :, :].rearrange("p h t -> p (h t)"), func=AF.Square), reads=["oT"], writes=["osq"])
        for hp in range(2):
            ps, pk = next_pf()

            def f(e):
                for j in range(2):
                    ins = e.matmul(ps[:, j * T:(j + 1) * T], lhsT=onesD[:, :], rhs=osq[:, 2 * hp + j, :], start=True, stop=True)
                return ins
            S.op("pe", f, reads=["osq", "onesD"], writes=[pk])
            rv = rso[:, 2 * hp:2 * hp + 2, :].rearrange("p h t -> p (h t)")
            S.op("act", lambda e: e.activation(out=rv, in_=ps[:, 0:2 * T], func=AF.Sqrt, bias=epsT[:, 0:1]), reads=[pk, "epsT"],
                 writes=[("rso", hp)])
            S.op("dve", lambda e: e.reciprocal(rv, rv), reads=[("rso", hp)], writes=[("rso", hp)])
        for h in range(4):
            S.op("dve", lambda e: e.scalar_tensor_tensor(out=otmp[:, h, :], in0=oT[:, h, :], scalar=L.glnT[:, h:h + 1], in1=rso[:, h, :],
                                                         op0=ALU.mult, op1=ALU.mult), reads=["oT", ("glnT", L.l), ("rso", h // 2)],
                 writes=["osq"])
            S.op("pool", lambda e: e.tensor_tensor(out=ymix[:, 4 + h, :], in0=otmp[:, h, :], in1=srT[:, h, :], op=ALU.mult),
                 reads=["osq", ("srT", h)], writes=[("ymix", 4 + h)])
        if stop == "onorm":
            raise _Stop()
        for blk in range(NB):
            for half in range(2):
                ps, pk = next_pf()

                def f(e):
                    for kc in range(8):
                        ins = e.matmul(ps[:, :], lhsT=ymix[:, kc, blk * 128:(blk + 1) * 128], rhs=woutb[:, kc, half * 512:(half + 1) * 512],
                                       start=(kc == 0), stop=(kc == 7))
                    return ins
                S.op("pe", f, reads=[("ymix", i) for i in range(8)] + ["woutb"], writes=[pk])
                S.op("dve", lambda e: e.tensor_tensor(out=xt[:, blk, half * 512:(half + 1) * 512], in0=xt[:, blk, half * 512:(half + 1) * 512],
                                                      in1=ps[:, :], op=ALU.add), reads=[pk, xk], writes=[xk])

    wctr = {"up": 0, "dn": 0}

    def ffn(t, xt, xk):
        norm_to_hT(xt, xk, 24, 32)
        pend = []

        def issue_up(i):
            s = wctr["up"] % 3
            wctr["up"] += 1
            S.dma("sp", "ld_wup%d" % s, out=wups[s][:, :, :], in_=L.wup_s[i], reads=["scratch0", "scratch1"], writes=[("wups", s)])
            return s

        def issue_dn(i0, gsz):
            s = wctr["dn"] % 2
            wctr["dn"] += 1
            S.dma("sp", "ld_wdn%d" % s, out=wdng[s][:, 0:gsz, :], in_=L.wdn_s[i0:i0 + gsz].rearrange("i p n -> p i n"),
                  reads=["scratch0", "scratch1"], writes=[("wdng", s)])
            return s

        slots = {}
        slots[0] = issue_up(0)
        slots[1] = issue_up(1)
        i0 = 0
        DEFER = 3
        pending = []

        def emit_down(gsz, hs, ds):
            for blk in range(NB):
                for half in range(2):
                    ps, pk = next_pf()

                    def f(e):
                        for jj in range(gsz):
                            ins = e.matmul(ps[:, :], lhsT=hidg[hs][:, jj, blk * 128:(blk + 1) * 128],
                                           rhs=wdng[ds][:, jj, half * 512:(half + 1) * 512], start=(jj == 0), stop=(jj == gsz - 1))
                        return ins
                    S.op("pe", f, reads=[("hidg", hs), ("wdng", ds)], writes=[pk])
                    S.op("dve", lambda e: e.tensor_tensor(out=xt[:, blk, half * 512:(half + 1) * 512],
                                                          in0=xt[:, blk, half * 512:(half + 1) * 512], in1=ps[:, :], op=ALU.add),
                         reads=[pk, xk], writes=[xk])

        for gi, gsz in enumerate(GROUPS):
            ds = issue_dn(i0, gsz)
            hs = gi % 2
            for jj in range(gsz):
                if jj == DEFER and pending:
                    emit_down(*pending.pop())
                i = i0 + jj
                if i + 2 < NPAIR:
                    slots[i + 2] = issue_up(i + 2)
                s = slots[i]
                res = []
                for ab in range(2):
                    j = i + ab * NPAIR
                    ps, pk = next_pf()

                    def f(e):
                        for k in range(8):
                            ins = e.matmul(ps[:, 0:T], lhsT=wups[s][:, k, ab * 128:(ab + 1) * 128], rhs=hT[:, k, :],
                                           start=(k == 0), stop=(k == 7))
                        return ins
                    S.op("pe", f, reads=hT_all + [("wups", s)], writes=[pk])
                    ui = (2 * i + ab) % 4
                    up = upb_[ui]
                    upk = ("up", ui)
                    tc_ = tcv[ui]
                    tk = ("tcv", ui)
                    S.op("pool", lambda e: e.tensor_copy(up[:, 0:2], L.chalo[:, j, :]), reads=[("chalo", L.l, j)], writes=[upk])
                    S.op("act", lambda e: e.activation(out=up[:, 2:2 + T], in_=ps[:, 0:T], func=AF.Copy), reads=[pk], writes=[upk])
                    S.op("pool", lambda e: e.tensor_copy(L.chalo[:, j, :], up[:, T:T + 2]), reads=[upk], writes=[("chalo", L.l, j)])
                    S.op("act", lambda e: e.activation(out=tc_[:, :], in_=ps[:, 0:T], func=AF.Identity,
                                                       scale=L.cwT[:, 3 * j + 2:3 * j + 3], bias=L.cbT[:, j:j + 1]),
                         reads=[pk, ("cwT", L.l), ("cbT", L.l)], writes=[tk])
                    S.op("dve", lambda e: e.scalar_tensor_tensor(out=tc_[:, :], in0=up[:, 1:1 + T], scalar=L.cwT[:, 3 * j + 1:3 * j + 2],
                                                                 in1=tc_[:, :], op0=ALU.mult, op1=ALU.add),
                         reads=[upk, tk, ("cwT", L.l)], writes=[tk])
                    S.op("dve", lambda e: e.scalar_tensor_tensor(out=tc_[:, :], in0=up[:, 0:T], scalar=L.cwT[:, 3 * j:3 * j + 1],
                                                                 in1=tc_[:, :], op0=ALU.mult, op1=ALU.add),
                         reads=[upk, tk, ("cwT", L.l)], writes=[tk])
                    res.append((tc_, tk))
                sa = sact[i % 2]
                sk_ = ("sact", i % 2)
                S.op("act", lambda e: e.activation(out=sa[:, :], in_=res[0][0][:, :], func=AF.Silu), reads=[res[0][1]], writes=[sk_])
                S.op("pool", lambda e: e.tensor_tensor(out=hidg[hs][:, jj, :], in0=sa[:, :], in1=res[1][0][:, :], op=ALU.mult),
                     reads=[sk_, res[1][1]], writes=[("hidg", hs)])
            pending.append((gsz, hs, ds))
            i0 += gsz
        emit_down(*pending.pop())

    def finish(t, xt, xk, st_tile=-2):
        if st_tile == -2:
            st_tile = t
        norm_stats(xt, xk)
        S.op("dve", lambda e: e.tensor_scalar(out=sfin[:, 0:NB], in0=rstd[:, 0:NB], scalar1=flags[:, 0:1], scalar2=flags[:, 1:2],
                                              op0=ALU.mult, op1=ALU.add), reads=["rstd", "flags"], writes=["sfin"])
        for blk in range(NB):
            S.op("dve", lambda e: e.scalar_tensor_tensor(out=xt[:, blk, :], in0=xt[:, blk, :], scalar=sfin[:, blk:blk + 1],
                                                         in1=fnrow[:, :], op0=ALU.mult, op1=ALU.mult),
                 reads=[xk, "sfin", "fnrow"], writes=[xk])
        if st_tile >= 0:
            S.dma("sp", "st_o%d" % (t % 2), out=out_d[st_tile * T:(st_tile + 1) * T, :].rearrange("(b p) f -> p b f", p=128),
                  in_=xt[:, :, :], reads=[xk], writes=[("outd", st_tile)])

    if wave:
        def load_mw(l_):
            q_ = LS[l_]
            S.dma("sp", "ld_mw0", out=winb[:, :, :].rearrange("p k n -> p (k n)"), in_=q_.win_s, reads=[("mws", l_, 0)], writes=["winb"])
            S.dma("sp", "ld_mw1", out=woutb[:, :, :].rearrange("p k n -> p (k n)"), in_=q_.wout_s, reads=[("mws", l_, 1)], writes=["woutb"])
            S.dma("sp", "ld_mw2", out=wpoolb[:, :, :].rearrange("p g d -> p (g d)"), in_=q_.wpool_s, reads=[("mws", l_, 2)],
                  writes=["wpoolb"])

        load_x(0)
        load_mw(0)
        for t in range(NT):
            xt, xk = xts[t % 2], ("xt", t % 2)
            if t + 1 < NT:
                load_x(t + 1)
            for l_ in range(2):
                setL(l_)
                mixer(t, xt, xk)
                load_mw(1 - l_)
                ffn(t, xt, xk)
            finish(t, xt, xk)
        S.barrier()
        return nc

    setL(0)
    if fused:
        send = [nc.dram_tensor("send%d" % i, [T, D], F32).ap() for i in range(2)]
        gath = [nc.dram_tensor("gath%d" % i, [2 * T, D], F32).ap() for i in range(2)]
        xa = sb("xa", [128, NB, D])
        xg = sb("xg", [128, NB, D])
        coefB = sb("coefB", [128, 64])
        faw = sb("faw", [128, 4])
        for g, w in enumerate(WINDOWS):
            S.op("dve", lambda e: e.tensor_scalar(out=faw[:, g:g + 1], in0=flags[:, 2:3], scalar1=1.0 / w, scalar2=None, op0=ALU.mult),
                 reads=["flags"], writes=[("faw", g)])
            S.op("dve", lambda e: e.tensor_scalar(out=coefB[:, g * 16:(g + 1) * 16], in0=invc[:, g * 16:(g + 1) * 16],
                                                  scalar1=flags[:, 3:4], scalar2=faw[:, g:g + 1], op0=ALU.mult, op1=ALU.add),
                 reads=["invc", "flags", ("faw", g)], writes=[("coefB", g)])
        xt, xk = xts[0], ("xt", 0)
        fl = lambda a: a[:, :, :].rearrange("p b f -> p (b f)")

        def load_xa(i):
            ti = min(i, NT - 1)
            S.dma("sp", "ld_xa", out=xa[:, :, :], in_=x_d[ti * T:(ti + 1) * T, :].rearrange("(b p) f -> p b f", p=128), writes=["xa"])

        load_xa(0)
        for i in range(NT + 1):
            S.op("dve", lambda e: e.tensor_scalar(out=fl(xt), in0=fl(xa), scalar1=flags[:, 2:3], scalar2=None, op0=ALU.mult),
                 reads=["xa", "flags"], writes=[xk])
            if i >= 1:
                gk = ("gath", (i - 1) % 2)
                S.dma("sp", "ld_xg", out=xg[:, :, :], in_=gath[(i - 1) % 2][0:T, :].rearrange("(b p) f -> p b f", p=128),
                      reads=[gk], writes=["xg"])
                S.op("dve", lambda e: e.scalar_tensor_tensor(out=fl(xt), in0=fl(xg), scalar=flags[:, 3:4], in1=fl(xt),
                                                             op0=ALU.mult, op1=ALU.add), reads=["xg", "flags", xk], writes=[xk])
            if i + 1 <= NT:
                load_xa(i + 1)
            mixer(i, xt, xk)
            ffn(i, xt, xk)
            finish(i, xt, xk, st_tile=i - 1)
            if i == 0:
                S.op("dve", lambda e: e.tensor_scalar(out=L.Sst[:, :, :].rearrange("p a b -> p (a b)"),
                                                      in0=L.Sst[:, :, :].rearrange("p a b -> p (a b)"), scalar1=flags[:, 2:3],
                                                      scalar2=None, op0=ALU.mult), reads=[("Sst", L.l), "flags"], writes=[("Sst", L.l)])
                S.op("dve", lambda e: e.tensor_scalar(out=L.ubuf[:, :, 0:16], in0=L.ubuf[:, :, 0:16], scalar1=flags[:, 2:3],
                                                      scalar2=None, op0=ALU.mult), reads=[("ub", L.l, g) for g in range(4)] + ["flags"],
                     writes=[("ub", L.l, g) for g in range(4)])
                S.op("dve", lambda e: e.tensor_scalar(out=L.chalo[:, :, :].rearrange("p a b -> p (a b)"),
                                                      in0=L.chalo[:, :, :].rearrange("p a b -> p (a b)"), scalar1=flags[:, 2:3],
                                                      scalar2=None, op0=ALU.mult), reads=[("chalo", L.l, j) for j in range(44)] + ["flags"],
                     writes=[("chalo", L.l, j) for j in range(44)])
            if i < NT:
                sk_ = ("send", i % 2)
                S.dma("sp", "st_send%d" % (i % 2), out=send[i % 2].rearrange("(b p) f -> p b f", p=128), in_=xt[:, :, :],
                      reads=[xk], writes=[sk_])
                S.custom("pool", "ag%d" % (i % 2),
                         lambda e: e.collective_compute("AllGather", ALU.bypass, replica_groups=groups, ins=[send[i % 2]],
                                                        outs=[gath[i % 2]]),
                         reads=[sk_], writes=[("gath", i % 2)])
        S.barrier()
        return nc

    load_x(0)
    for t in range(NT):
        xt, xk = xts[t % 2], ("xt", t % 2)
        if t + 1 < NT:
            load_x(t + 1)
        if stop == "norm":
            norm_to_hT(xt, xk, 0, 8)
            S.barrier()
            dump("hT", hT[:, :, :], [128, 8, T], BF16)
            dump("rstd", rstd[:, :], [128, 4])
            S.barrier()
            return nc
        try:
            mixer(t, xt, xk)
        except _Stop:
            S.barrier()
            dump("xt", xt[:, :, :], [128, NB, D])
            S.barrier()
            return nc
        if stop == "mixer":
            S.barrier()
            dump("xt", xt[:, :, :], [128, NB, D])
            dump("ymix", ymix[:, :, :], [128, 8, T], BF16)
            dump("oT", oT[:, :, :], [128, 4, T])
            dump("qpp", qpp[:, :, :], [128, 4, T], BF16)
            dump("kpp", kpp[:, :, :], [128, 2, T], BF16)
            dump("gtm", gtm[:, :, :], [128, NB, 256])
            dump("srT", srT[:, :, :], [128, 4, T])
            dump(("Sst", L.l), L.Sst[:, :, :], [128, 2, 256])
            dump("vb", vb[:, :, :], [128, NB, 512], BF16)
            S.barrier()
            return nc
        ffn(t, xt, xk)
        finish(t, xt, xk)
    S.barrier()
    return nc


def _consts():
    tri = np.triu(np.ones((128, 128), np.float32))
    invc = np.zeros((128, 64), np.float32)
    for g, w in enumerate(WINDOWS):
        for t in range(16):
            invc[:, g * 16 + t] = 1.0 / min(t + 1, w)
    hmk = np.zeros((128, 2), np.float32)
    hmk[:64, 0] = 1.0
    hmk[64:, 1] = 1.0
    return {"hmask": hmk, "tri": tri, "maskrep": np.ascontiguousarray(np.tile(tri, (1, 4))), "ident": np.eye(128, dtype=np.float32), "invcnt": invc}


def _core_inputs(xin, b, l, final, P):
    f32 = np.float32
    tp = lambda v, n: np.ascontiguousarray(np.asarray(v, f32).reshape(n, 128).T)
    wg2a = np.zeros((32, 256), f32)
    wg2a[0:16] = P["w_gate2"][l]
    wg2a[16] = P["b_gate"][l]
    ab = np.asarray(P["ada_b"][l], f32)
    m = {
        "x": np.ascontiguousarray(xin, dtype=f32),
        "cT": tp(P["c"][b], 8),
        "ada_w": np.ascontiguousarray(P["ada_w"][l], dtype=f32),
        "ada_bT": tp(ab, 48),
        "ada_bg": np.ascontiguousarray(np.concatenate([ab[2048:3072], ab[5120:6144]])[None, :]),
        "w_in": np.ascontiguousarray(P["w_in"][l], dtype=f32),
        "wg2a": wg2a,
        "w_pool": np.ascontiguousarray(np.transpose(np.asarray(P["w_pool"][l], f32), (1, 0, 2))),
        "pool_scT": tp(P["pool_scale"][l], 4),
        "gla_nT": tp(P["gla_norm"][l], 4),
        "w_out": np.ascontiguousarray(P["w_out"][l], dtype=f32),
        "w_up": np.ascontiguousarray(P["w_up"][l], dtype=f32),
        "conv_wT": np.ascontiguousarray(np.transpose(np.asarray(P["conv_w"][l], f32).reshape(3, 44, 128), (2, 1, 0)).reshape(128, 132)),
        "conv_bT": tp(P["conv_b"][l], 44),
        "w_down": np.ascontiguousarray(P["w_down"][l], dtype=f32),
        "fnrow": np.ascontiguousarray(np.broadcast_to(np.asarray(P["final_norm"], f32)[None, :], (128, D))) if final
        else np.ones((128, D), f32),
        "flags": np.ascontiguousarray(np.broadcast_to(np.array(([1.0, 0.0] if final else [0.0, 1.0]) +
                                                                ([1.0, 0.0] if l == 0 else [0.0, 1.0]), f32)[None, :], (128, 4))),
    }
    m.update(_consts())
    return m


_PER_LAYER = ("ada_w", "ada_bT", "ada_bg", "w_in", "wg2a", "w_pool", "pool_scT", "gla_nT", "w_out", "w_up", "conv_wT", "conv_bT",
              "w_down")


def _core_inputs_wave(xin, b, P):
    m0 = _core_inputs(xin, b, 0, True, P)
    m1 = _core_inputs(xin, b, 1, True, P)
    m = {k: v for k, v in m1.items() if k not in _PER_LAYER}
    for k in _PER_LAYER:
        m[k + "0"] = m0[k]
        m[k + "1"] = m1[k]
    return m


_NC_CACHE = {}


def _get_nc(NT):
    if NT not in _NC_CACHE:
        _NC_CACHE[NT] = build_program(NT)
    return _NC_CACHE[NT]


def run_layer(xs, l, final, P, NT):
    nc = build_program(NT)
    in_maps = [_core_inputs(xs[b], b, l, final, P) for b in range(len(xs))]
    in_maps = in_maps + in_maps
    res = run_bass_kernel_spmd(nc, in_maps, core_ids=list(range(8)))
    return [np.asarray(r["out"]) for r in res.results[:len(xs)]]


def kernel(**inputs):
    P = {k: np.asarray(v) for k, v in inputs.items()}
    x = P["x"]
    NT = SEQ // T
    nc = build_program(NT, wave=True)
    in_maps = [_core_inputs_wave(x[b], b, P) for b in range(BATCH)]
    in_maps = in_maps + in_maps
    res = run_bass_kernel_spmd(nc, in_maps, core_ids=list(range(8)))
    return np.stack([np.asarray(res.results[b]["out"]) for b in range(BATCH)], axis=0).astype(np.float32)
```
